# Optimizing a Trainium2 kernel written in Bass

```python
import jax
import jax.numpy as jnp
from jax import lax

D_MODEL = 1024
BATCH = 8
SEQ = 4096
DEPTH = 4

CTX_LEN = 256
GRID_W = 64
N_MIXERS = 2
N_NA_LAYERS = (DEPTH + N_MIXERS - 1) // N_MIXERS
N_LRU_LAYERS = DEPTH // N_MIXERS
N_HEADS = 16
HEAD_DIM = D_MODEL // N_HEADS
WIN_ROWS = 8
WIN_COLS = 16
Q_BLOCK_W = 16
K_BLOCK_W = Q_BLOCK_W + WIN_COLS
D_RNN = D_MODEL
LRU_BLOCKS = 16
LRU_BLOCK_W = D_RNN // LRU_BLOCKS
LRU_CONV_W = 4
LRU_C = 8.0
D_FF = 3 * D_MODEL
FFN_CONV_W = 3
N_MODS = 6
EPS = 1e-6

kernel_name = 'hybrid_na_rglru_diffusion_block'


def rms_norm(t, gain):
    tf = t.astype(jnp.float32)
    y = tf * lax.rsqrt(jnp.mean(tf * tf, axis=-1, keepdims=True) + EPS)
    return (y * gain.astype(jnp.float32)).astype(t.dtype)


def modulate(t, gain, shift, scale):
    return rms_norm(t, gain) * (1.0 + scale[:, None, :]) + shift[:, None, :]


def depthwise_conv_centred(t, w, b):
    width, length = w.shape[0], t.shape[1]
    left = width // 2
    tp = jnp.pad(t, ((0, 0), (left, width - 1 - left), (0, 0)))
    y = b + tp[:, 0:length] * w[0]
    for k in range(1, width):
        y = y + tp[:, k:k + length] * w[k]
    return y


def neighbourhood_attention(q, k, v, kc, vc, rpb):
    B, S, H, Dh = q.shape
    rows = S // GRID_W
    wr = min(WIN_ROWS, rows)
    n_cb = GRID_W // Q_BLOCK_W
    scale = Dh ** -0.5
    qg = q.reshape(B, rows, GRID_W, H, Dh)
    kg = k.reshape(B, rows, GRID_W, H, Dh)
    vg = v.reshape(B, rows, GRID_W, H, Dh)
    n_lat = wr * K_BLOCK_W

    def attend_block(idx):
        r = idx // n_cb
        q0 = (idx % n_cb) * Q_BLOCK_W
        rs = jnp.clip(r - WIN_ROWS // 2, 0, rows - wr)
        ks = jnp.clip(q0 - WIN_COLS // 2, 0, GRID_W - K_BLOCK_W)
        qb = lax.dynamic_slice(qg, (0, r, q0, 0, 0), (B, 1, Q_BLOCK_W, H, Dh))[:, 0]
        kb = lax.dynamic_slice(kg, (0, rs, ks, 0, 0), (B, wr, K_BLOCK_W, H, Dh)).reshape(B, n_lat, H, Dh)
        vb = lax.dynamic_slice(vg, (0, rs, ks, 0, 0), (B, wr, K_BLOCK_W, H, Dh)).reshape(B, n_lat, H, Dh)
        q_col = q0 + jnp.arange(Q_BLOCK_W)
        win_start = jnp.clip(q_col - WIN_COLS // 2, 0, GRID_W - WIN_COLS)
        k_col = ks + jnp.arange(K_BLOCK_W)
        col_ok = (k_col[None, :] >= win_start[:, None]) & (k_col[None, :] < win_start[:, None] + WIN_COLS)
        mask = jnp.broadcast_to(col_ok[:, None, :], (Q_BLOCK_W, wr, K_BLOCK_W)).reshape(Q_BLOCK_W, n_lat)
        d_row = rs + jnp.arange(wr) - r + (WIN_ROWS - 1)
        d_col = jnp.clip(k_col[None, :] - q_col[:, None] + (WIN_COLS - 1), 0, 2 * WIN_COLS - 2)
        bias = rpb[:, d_row[None, :, None], d_col[:, None, :]].reshape(H, Q_BLOCK_W, n_lat)
        s_lat = jnp.einsum('bqhd,bkhd->bhqk', qb, kb).astype(jnp.float32) * scale + bias.astype(jnp.float32)[None]
        s_lat = jnp.where(mask[None, None], s_lat, -jnp.inf)
        s_ctx = jnp.einsum('bqhd,bkhd->bhqk', qb, kc).astype(jnp.float32) * scale
        p = jax.nn.softmax(jnp.concatenate([s_lat, s_ctx], axis=-1), axis=-1).astype(v.dtype)
        return (jnp.einsum('bhqk,bkhd->bqhd', p[..., :n_lat], vb)
                + jnp.einsum('bhqk,bkhd->bqhd', p[..., n_lat:], vc))

    out = lax.map(attend_block, jnp.arange(rows * n_cb))
    out = out.reshape(rows, n_cb, B, Q_BLOCK_W, H, Dh).transpose(2, 0, 1, 3, 4, 5)
    return out.reshape(B, S, H * Dh)


def context_attention(qc, kc, vc):
    s = jnp.einsum('bqhd,bkhd->bhqk', qc, kc).astype(jnp.float32) * (qc.shape[-1] ** -0.5)
    p = jax.nn.softmax(s, axis=-1).astype(vc.dtype)
    o = jnp.einsum('bhqk,bkhd->bqhd', p, vc)
    return o.reshape(qc.shape[0], qc.shape[1], -1)


def na_mixer(h, hc, w_qkv, q_gain, k_gain, rpb, w_out, with_ctx):
    B, S, D = h.shape
    qkv = (h @ w_qkv).reshape(B, S, 3, N_HEADS, HEAD_DIM)
    q = rms_norm(qkv[:, :, 0], q_gain)
    k = rms_norm(qkv[:, :, 1], k_gain)
    v = qkv[:, :, 2]
    C = hc.shape[1]
    if with_ctx:
        qkv_c = (hc @ w_qkv).reshape(B, C, 3, N_HEADS, HEAD_DIM)
        qc = rms_norm(qkv_c[:, :, 0], q_gain)
        kc = rms_norm(qkv_c[:, :, 1], k_gain)
        vc = qkv_c[:, :, 2]
    else:
        kv_c = (hc @ w_qkv[:, D:]).reshape(B, C, 2, N_HEADS, HEAD_DIM)
        kc = rms_norm(kv_c[:, :, 0], k_gain)
        vc = kv_c[:, :, 1]
    y = neighbourhood_attention(q, k, v, kc, vc, rpb) @ w_out
    yc = context_attention(qc, kc, vc) @ w_out if with_ctx else None
    return y, yc


def block_diag_linear(t, w, b):
    tb = t.reshape(t.shape[:-1] + (LRU_BLOCKS, LRU_BLOCK_W))
    return jnp.einsum('blnd,nde->blne', tb, w).reshape(t.shape) + b


def _lru_combine(left, right):
    a_l, b_l = left
    a_r, b_r = right
    return a_l * a_r, a_r * b_l + b_r


def rglru(xr, wa, ba, wx, bx, lam, h0, reverse):
    xf = xr.astype(jnp.float32)
    r = jax.nn.sigmoid(block_diag_linear(xf, wa.astype(jnp.float32), ba.astype(jnp.float32)))
    i = jax.nn.sigmoid(block_diag_linear(xf, wx.astype(jnp.float32), bx.astype(jnp.float32)))
    log_a = -LRU_C * r * jax.nn.softplus(-lam.astype(jnp.float32))
    a = jnp.exp(log_a)
    b = jnp.sqrt(-jnp.expm1(2.0 * log_a)) * (i * xf)
    if h0 is not None:
        first = -1 if reverse else 0
        b = b.at[:, first].add(a[:, first] * h0.astype(jnp.float32))
    _, hs = lax.associative_scan(_lru_combine, (a, b), axis=1, reverse=reverse)
    return hs.astype(xr.dtype)


def lru_mixer(h, hc, w_in, conv_w, conv_b, ga_w, ga_b, gx_w, gx_b, lam, w_out, with_ctx):
    gate, rec = jnp.split(h @ w_in, 2, axis=-1)
    xr = depthwise_conv_centred(rec, conv_w, conv_b)
    if with_ctx:
        gate_c, rec_c = jnp.split(hc @ w_in, 2, axis=-1)
    else:
        rec_c = hc @ w_in[:, D_RNN:]
    xr_c = depthwise_conv_centred(rec_c, conv_w, conv_b)
    hs_c_f = rglru(xr_c, ga_w[0], ga_b[0], gx_w[0], gx_b[0], lam[0], None, False)
    hs_f = rglru(xr, ga_w[0], ga_b[0], gx_w[0], gx_b[0], lam[0], hs_c_f[:, -1], False)
    hs_c_b = rglru(xr_c, ga_w[1], ga_b[1], gx_w[1], gx_b[1], lam[1], None, True)
    hs_b = rglru(xr, ga_w[1], ga_b[1], gx_w[1], gx_b[1], lam[1], hs_c_b[:, 0], True)
    y = ((hs_f + hs_b) * jax.nn.gelu(gate, approximate=True)) @ w_out
    yc = ((hs_c_f + hs_c_b) * jax.nn.gelu(gate_c, approximate=True)) @ w_out if with_ctx else None
    return y, yc


def conv_ffn(t, w_up, conv_w, conv_b, w_down):
    u = depthwise_conv_centred(t @ w_up, conv_w, conv_b)
    val, gate = jnp.split(u, 2, axis=-1)
    return (val * jax.nn.silu(gate)) @ w_down


def setup_inputs(seed: int = 0) -> dict:
    key = jax.random.key(seed)
    ks = jax.random.split(key, 32)
    f32 = jnp.float32

    def nrm(k, shape, scale):
        return jax.random.normal(k, shape, f32) * scale

    L, NA, NL = DEPTH, N_NA_LAYERS, N_LRU_LAYERS
    u = jax.random.uniform(ks[20], (NL, 2, D_RNN), f32, 0.9, 0.999)
    a_base = u ** (1.0 / LRU_C)
    lam = jnp.log(a_base) - jnp.log1p(-a_base)
    return {
        'x': nrm(ks[0], (BATCH, SEQ, D_MODEL), 1.0),
        'c': nrm(ks[1], (BATCH, D_MODEL), 1.0),
        'ctx': nrm(ks[2], (BATCH, CTX_LEN, D_MODEL), 1.0),
        'c_ctx': nrm(ks[3], (D_MODEL,), 1.0),
        'ada_w': nrm(ks[4], (L, D_MODEL, N_MODS * D_MODEL), 0.5 * D_MODEL ** -0.5),
        'ada_b': nrm(ks[5], (L, N_MODS * D_MODEL), 0.02),
        'norm_mix': 1.0 + nrm(ks[6], (L, D_MODEL), 0.02),
        'norm_ffn': 1.0 + nrm(ks[7], (L, D_MODEL), 0.02),
        'na_w_qkv': nrm(ks[8], (NA, D_MODEL, 3 * D_MODEL), D_MODEL ** -0.5),
        'na_q_gain': 1.0 + nrm(ks[9], (NA, HEAD_DIM), 0.02),
        'na_k_gain': 1.0 + nrm(ks[10], (NA, HEAD_DIM), 0.02),
        'na_rpb': nrm(ks[11], (NA, N_HEADS, 2 * WIN_ROWS - 1, 2 * WIN_COLS - 1), 0.5),
        'na_w_out': nrm(ks[12], (NA, D_MODEL, D_MODEL), D_MODEL ** -0.5),
        'lru_w_in': nrm(ks[13], (NL, D_MODEL, 2 * D_RNN), D_MODEL ** -0.5),
        'lru_conv_w': nrm(ks[14], (NL, LRU_CONV_W, D_RNN), LRU_CONV_W ** -0.5),
        'lru_conv_b': nrm(ks[15], (NL, D_RNN), 0.02),
        'lru_ga_w': nrm(ks[16], (NL, 2, LRU_BLOCKS, LRU_BLOCK_W, LRU_BLOCK_W), LRU_BLOCK_W ** -0.5),
        'lru_ga_b': nrm(ks[17], (NL, 2, D_RNN), 0.02),
        'lru_gx_w': nrm(ks[18], (NL, 2, LRU_BLOCKS, LRU_BLOCK_W, LRU_BLOCK_W), LRU_BLOCK_W ** -0.5),
        'lru_gx_b': nrm(ks[19], (NL, 2, D_RNN), 0.02),
        'lru_lambda': lam,
        'lru_w_out': nrm(ks[21], (NL, D_RNN, D_MODEL), D_RNN ** -0.5),
        'ffn_w_up': nrm(ks[22], (L, D_MODEL, 2 * D_FF), D_MODEL ** -0.5),
        'ffn_conv_w': nrm(ks[23], (L, FFN_CONV_W, 2 * D_FF), FFN_CONV_W ** -0.5),
        'ffn_conv_b': nrm(ks[24], (L, 2 * D_FF), 0.02),
        'ffn_w_down': nrm(ks[25], (L, D_FF, D_MODEL), D_FF ** -0.5),
    }


def reference(x, c, ctx, c_ctx, ada_w, ada_b, norm_mix, norm_ffn,
              na_w_qkv, na_q_gain, na_k_gain, na_rpb, na_w_out,
              lru_w_in, lru_conv_w, lru_conv_b, lru_ga_w, lru_ga_b,
              lru_gx_w, lru_gx_b, lru_lambda, lru_w_out,
              ffn_w_up, ffn_conv_w, ffn_conv_b, ffn_w_down):
    s_lat = jax.nn.silu(c)
    s_ctx = jax.nn.silu(c_ctx)[None]
    for i in range(DEPTH):
        last = i == DEPTH - 1
        j = i // N_MIXERS
        sh1, sc1, g1, sh2, sc2, g2 = jnp.split(s_lat @ ada_w[i] + ada_b[i], N_MODS, axis=-1)
        csh1, csc1, cg1, csh2, csc2, cg2 = jnp.split(s_ctx @ ada_w[i] + ada_b[i], N_MODS, axis=-1)
        h = modulate(x, norm_mix[i], sh1, sc1)
        hc = modulate(ctx, norm_mix[i], csh1, csc1)
        if i % N_MIXERS == 0:
            y, yc = na_mixer(h, hc, na_w_qkv[j], na_q_gain[j], na_k_gain[j], na_rpb[j],
                             na_w_out[j], not last)
        else:
            y, yc = lru_mixer(h, hc, lru_w_in[j], lru_conv_w[j], lru_conv_b[j],
                              lru_ga_w[j], lru_ga_b[j], lru_gx_w[j], lru_gx_b[j],
                              lru_lambda[j], lru_w_out[j], not last)
        x = x + g1[:, None] * y
        h = modulate(x, norm_ffn[i], sh2, sc2)
        x = x + g2[:, None] * conv_ffn(h, ffn_w_up[i], ffn_conv_w[i], ffn_conv_b[i], ffn_w_down[i])
        if not last:
            ctx = ctx + cg1[:, None] * yc
            hc = modulate(ctx, norm_ffn[i], csh2, csc2)
            ctx = ctx + cg2[:, None] * conv_ffn(hc, ffn_w_up[i], ffn_conv_w[i], ffn_conv_b[i], ffn_w_down[i])
    return x
```

```python
import math
from contextlib import ExitStack

import numpy as np
import ml_dtypes
import concourse.bass as bass
import concourse.mybir as mybir
from concourse.bass_utils import run_bass_kernel_spmd

F32 = mybir.dt.float32
BF16 = mybir.dt.bfloat16
ALU = mybir.AluOpType
AF = mybir.ActivationFunctionType

D = 1024
NTOK = 4352
NCTX = 256
DEPTH = 4
EPS = 1e-6
LW = 4358


class Buf:
    __slots__ = ("name", "w", "r", "grp")

    def __init__(self, name, grp=None):
        self.name = name
        self.w = {}
        self.r = {}
        self.grp = grp or name


class Prog:
    ENG = ("pe", "act", "dve", "pool", "sp")

    def __init__(self, nc, stack):
        self.nc = nc
        self.stack = stack
        self.q = {e: [] for e in self.ENG}
        self.sem = {}
        self.cnt = {}
        self.waited = {e: {} for e in self.ENG}
        for e in ("pe", "act", "dve", "pool"):
            self._mksem("E_" + e)

    def _mksem(self, key):
        if key not in self.sem:
            self.sem[key] = self.stack.enter_context(self.nc.semaphore("s_" + key))
            self.cnt[key] = 0
        return key

    def _waits(self, qn, reads, writes, is_dma, waw=True):
        own = None if is_dma else "E_" + qn
        w = {}

        def merge(d, skip_own):
            for k, v in d.items():
                if skip_own and k == own:
                    continue
                if v > w.get(k, 0):
                    w[k] = v
        for b in reads:
            merge(b.w, False)
        for b in writes:
            if waw:
                merge(b.w, True)
            merge(b.r, True)
        out = []
        wd = self.waited[qn]
        for k, v in w.items():
            if v > wd.get(k, 0):
                wd[k] = v
                out.append((k, v))
        return out

    def op(self, qn, fns, reads=(), writes=()):
        if callable(fns):
            fns = [fns]
        waits = self._waits(qn, reads, writes, False)
        key = "E_" + qn
        self.cnt[key] += 1
        val = self.cnt[key]
        self.q[qn].append((waits, fns, (key, 1)))
        for b in reads:
            b.r[key] = val
        for b in writes:
            b.w[key] = val
            b.r = {}

    def dma(self, qn, fns, reads=(), writes=(), waw=True, semgrp=None):
        if callable(fns):
            fns = [fns]
        waits = self._waits(qn, reads, writes, True, waw)
        key = self._mksem("D_" + (semgrp or writes[0].grp))
        self.cnt[key] += 16 * len(fns)
        val = self.cnt[key]
        self.q[qn].append((waits, fns, (key, 16, True)))
        for b in reads:
            b.r[key] = val
        for b in writes:
            b.w[key] = val
            b.r = {}

    def final_wait(self, qn, bufs):
        waits = self._waits(qn, bufs, (), True)
        self.q[qn].append((waits, [], None))

    def emit(self):
        nc = self.nc
        sem = self.sem

        def run(eng, items):
            for waits, fns, sig in items:
                for k, v in waits:
                    eng.wait_ge(sem[k], v)
                n = len(fns)
                for i, fn in enumerate(fns):
                    r = fn(eng)
                    if sig is not None and (len(sig) == 3 or i == n - 1):
                        r.then_inc(sem[sig[0]], sig[1])
        with nc.Block() as block:
            @block.tensor
            def _(e):
                run(e, self.q["pe"])

            @block.scalar
            def _(e):
                run(e, self.q["act"])

            @block.vector
            def _(e):
                run(e, self.q["dve"])

            @block.gpsimd
            def _(e):
                run(e, self.q["pool"])

            @block.sync
            def _(e):
                run(e, self.q["sp"])


def TT(out, in0, in1, op):
    return lambda e: e.tensor_tensor(out=out, in0=in0, in1=in1, op=op)


def TS(out, in0, s1, s2, op0, op1=None):
    if op1 is None:
        return lambda e: e.tensor_scalar(out=out, in0=in0, scalar1=s1, scalar2=None, op0=op0)
    return lambda e: e.tensor_scalar(out=out, in0=in0, scalar1=s1, scalar2=s2, op0=op0, op1=op1)


def STT(out, in0, scalar, in1, op0, op1):
    return lambda e: e.scalar_tensor_tensor(out=out, in0=in0, scalar=scalar, in1=in1, op0=op0, op1=op1)


def ACTF(out, in_, func, bias=0.0, scale=1.0):
    return lambda e: e.activation(out=out, in_=in_, func=func, bias=bias, scale=scale)


def MM(out, lhsT, rhs, start=True, stop=True):
    return lambda e: e.matmul(out, lhsT=lhsT, rhs=rhs, start=start, stop=stop)


def CP(out, in_):
    return lambda e: e.tensor_copy(out=out, in_=in_)


def RCP(out, in_):
    return lambda e: e.reciprocal(out=out, in_=in_)


def MSET(ap, v):
    return lambda e: e.memset(ap, v)


def DMA(out, in_):
    return lambda e: e.dma_start(out=out, in_=in_)


def SCAN(out, d0, d1, init):
    return lambda e: e.tensor_tensor_scan(out=out, data0=d0, data1=d1, initial=init, op0=ALU.mult, op1=ALU.add)


class Pipe:
    def __init__(self):
        self.items = []
        self.t = 0
        self.n = 0

    def defer(self, delay, fn):
        self.items.append((self.t + delay, self.n, fn))
        self.n += 1

    def tick(self):
        self.t += 1
        ready = sorted([it for it in self.items if it[0] <= self.t])
        self.items = [it for it in self.items if it[0] > self.t]
        for _, _, fn in ready:
            fn()

    def flush(self):
        while self.items:
            self.tick()


TILES256 = [(0, 256, 1)] + [(256 + 256 * j, 256, 0) for j in range(16)]
TILES512 = [(0, 256, 1)] + [(256 + 512 * j, 512, 0) for j in range(8)]


def colp(tok):
    return tok + 2 if tok < NCTX else tok + 4


def build_program(nlayers=DEPTH):
    nc = bass.Bass("TRN2", target_bir_lowering=False)

    def din(name, shape, dt=F32):
        return nc.dram_tensor(name, list(shape), dt, kind="ExternalInput").ap()

    xin = din("xin", [D, NTOK])
    ccol = din("ccol", [128, 8, 2])
    ada_w = din("ada_w", [DEPTH, D, 6 * D])
    ada_b = din("ada_b", [128, DEPTH, 48])
    nmix = din("nmix", [128, DEPTH, 8])
    nffn = din("nffn", [128, DEPTH, 8])
    w_qkv = din("w_qkv", [2, D, 3 * D])
    qkg = din("qkg", [128, 2, 2])
    rpbp = din("rpbp", [2, 16, 15, 127])
    w_nao = din("w_nao", [2, D, D])
    w_lin = din("w_lin", [2, D, 2 * D])
    lcw = din("lcw", [128, 2, 4, 8])
    lcb = din("lcb", [128, 2, 8])
    lbd = din("lbd", [128, 2, 2, 2, 8, 128])
    lgb = din("lgb", [128, 2, 2, 2, 8])
    llam = din("llam", [128, 2, 2, 8])
    w_lout = din("w_lout", [2, D, D])
    w_up = din("w_up", [DEPTH, D, 6 * D])
    fcw = din("fcw", [128, DEPTH, 3, 48])
    fcb = din("fcb", [128, DEPTH, 48])
    w_down = din("w_down", [DEPTH, 3 * D, D])
    c_ident = din("c_ident", [128, 128], BF16)
    c_bones = din("c_bones", [128, 128], BF16)
    c_ones = din("c_ones", [128, 128], BF16)
    c_cm = din("c_cm", [128, 64])
    out = nc.dram_tensor("out", [D, NTOK - NCTX], F32, kind="ExternalOutput").ap()
    XA = nc.dram_tensor("XA", [D, NTOK], F32, kind="Internal").ap()
    XB = nc.dram_tensor("XB", [D, NTOK], F32, kind="Internal").ap()
    YT = nc.dram_tensor("YT", [8, 128, NTOK], BF16, kind="Internal").ap()
    GS = nc.dram_tensor("GS", [24, 128, NTOK], BF16, kind="Internal").ap()
    bXA, bXB, bYT, bGS, bOUT = Buf("XA"), Buf("XB"), Buf("YT"), Buf("GS"), Buf("OUT")

    with ExitStack() as st:
        P = Prog(nc, st)

        def sb(name, shape, dt=F32):
            return st.enter_context(nc.sbuf_tensor(name, list(shape), dt))

        HT = sb("HT", [128, 8, NTOK], BF16)
        bHT = [Buf("HT%d" % i) for i in range(len(TILES256))]
        LN = sb("LN", [128, 5, LW], F32)
        bL = [Buf("L%d" % i) for i in range(5)]
        XS = sb("XS", [128, 2, 2176], F32)
        bXS = [Buf("XS0"), Buf("XS1")]
        WS = sb("WS", [128, 2, 1024], F32)
        bWS = [Buf("WS%d" % i) for i in range(2)]
        WB = sb("WB", [128, 2, 3072], BF16)
        bWB = [Buf("WB0"), Buf("WB1")]
        ident = sb("ident", [128, 128], BF16)
        bones = sb("bones", [128, 128], BF16)
        ones = sb("ones", [128, 128], BF16)
        cm = sb("cm", [128, 64], F32)
        bconst = Buf("const")
        MODS = sb("MODS", [128, 48, 2], F32)
        bMODS = Buf("MODS")
        DER = sb("DER", [128, 2, 6, 2, 8], F32)
        bDER = [Buf("DER0"), Buf("DER1")]
        S2 = sb("S2", [128, 8, 2], F32)
        bS2 = Buf("S2")
        ADAB = sb("ADAB", [128, DEPTH, 48], F32)
        NMIX = sb("NMIX", [128, DEPTH, 8], F32)
        NFFN = sb("NFFN", [128, DEPTH, 8], F32)
        QKG = sb("QKG", [128, 2, 2], F32)
        LCW = sb("LCW", [128, 2, 4, 8], F32)
        LCB = sb("LCB", [128, 2, 8], F32)
        LGB = sb("LGB", [128, 2, 2, 2, 8], F32)
        LLAM = sb("LLAM", [128, 2, 2, 8], F32)
        LC8 = sb("LC8", [128, 2, 2, 8, 2], F32)
        LGBH = sb("LGBH", [128, 2, 2, 2, 8], F32)
        FCW = sb("FCW", [128, 3, 48], F32)
        FCB = sb("FCB", [128, 48], F32)
        bFC = Buf("FC")
        bsmall = Buf("small")
        TMPA = sb("TMPA", [128, 2, 512], F32)
        bTA = [Buf("TA0"), Buf("TA1")]
        TMPB = sb("TMPB", [128, 2, 512], BF16)
        bTB = [Buf("TB0"), Buf("TB1")]
        PT = sb("PT", [128, 2, 896], BF16)
        bPT = [Buf("PT0"), Buf("PT1")]
        AT = sb("AT", [128, 2, 128], BF16)
        bAT = [Buf("AT0"), Buf("AT1")]
        RC = sb("RC", [128, 2, 2], F32)
        bRC = [Buf("RC0"), Buf("RC1")]
        PSALL = st.enter_context(nc.psum_tensor("psall", [128, 7 * 512], F32))
        PSB = [PSALL[:, i * 512:(i + 1) * 512] for i in range(7)]
        bPS = [Buf("ps%d" % i) for i in range(7)]
        PTR = st.enter_context(nc.psum_tensor("ptr", [128, 1024], BF16))
        bPTR = [Buf("ptr%d" % i) for i in range(8)]
        rr = {"ps": 0, "ws": 0, "ta": 0, "tb": 0, "tr": 0, "pt": 0, "ss": 0, "oi": 0}

        def nxt(key, n):
            v = rr[key]
            rr[key] = (v + 1) % n
            return v

        def psbank(lo=0, hi=7):
            key = "ps%d_%d" % (lo, hi)
            if key not in rr:
                rr[key] = 0
            i = lo + nxt(key, hi - lo)
            return PSB[i], bPS[i]

        def ld(dst, src, b):
            P.dma("sp", DMA(dst, src), writes=[b])
        ld(ident[:], c_ident, bconst)
        ld(bones[:], c_bones, bconst)
        ld(ones[:], c_ones, bconst)
        ld(cm[:], c_cm, bconst)
        for dst, src in ((ADAB, ada_b), (NMIX, nmix), (NFFN, nffn), (QKG, qkg), (LCW, lcw), (LCB, lcb),
                         (LGB, lgb), (LLAM, llam)):
            ld(dst[:], src, bsmall)
        ld(S2[:], ccol, bS2)
        P.op("act", ACTF(S2[:], S2[:], AF.Silu), reads=[bS2], writes=[bS2])
        P.op("dve", TS(QKG[:, :, 0:1], QKG[:, :, 0:1], 0.125, None, ALU.mult), reads=[bsmall], writes=[bsmall])
        P.op("act", ACTF(LLAM[:], LLAM[:], AF.Exp, scale=-1.0), reads=[bsmall], writes=[bsmall])
        P.op("act", ACTF(LLAM[:], LLAM[:], AF.Ln, bias=1.0), reads=[bsmall], writes=[bsmall])
        P.op("dve", TS(LC8[:, :, :, :, 0], LLAM[:], -4.0, None, ALU.mult), reads=[bsmall], writes=[bsmall])
        P.op("dve", TS(LC8[:, :, :, :, 1], LLAM[:], -8.0, None, ALU.mult), reads=[bsmall], writes=[bsmall])
        P.op("dve", TS(LGBH[:], LGB[:], 0.5, None, ALU.mult), reads=[bsmall], writes=[bsmall])

        def load_w(src_ap, dst_ap, dst_bufs, n):
            s = nxt("ws", 2)
            shp = src_ap.shape
            if len(shp) == 3:
                stage = WS[:, s, 0:n].rearrange("p (a b) -> p a b", a=shp[1])
            else:
                stage = WS[:, s, 0:n]
            P.dma("sp", DMA(stage, src_ap), writes=[bWS[s]])
            P.op("pool", CP(dst_ap, stage), reads=[bWS[s]], writes=dst_bufs)

        def wview(w2d, c0, ncols):
            return w2d.rearrange("(k p) n -> p k n", p=128)[:, :, c0:c0 + ncols]

        def mods_items(i):
            par = i % 2
            ps, bps = PSB[6], bPS[6]
            items = []

            slot = {}

            def piece_dma(mc):
                s_ = nxt("ws", 2)
                slot[mc] = s_
                stage = WS[:, s_, :].rearrange("p (k n) -> p k n", k=8)
                P.dma("sp", DMA(stage, wview(ada_w[i], mc * 128, 128)), writes=[bWS[s_]])

            def piece_mm(mc):
                s_ = slot[mc]
                stage = WS[:, s_, :].rearrange("p (k n) -> p k n", k=8)
                P.op("pe", [MM(ps[:, mc * 2:mc * 2 + 2], stage[:, k, :], S2[:, k, :], k == 0, k == 7) for k in range(8)],
                     reads=[bWS[s_], bS2], writes=[bps])

            def fin():
                psv = ps[:, 0:96].rearrange("p (m t) -> p m t", t=2)
                P.op("dve", TT(MODS[:], psv, ADAB[:, i, :].unsqueeze(2).to_broadcast([128, 48, 2]), ALU.add),
                     reads=[bps, bsmall], writes=[bMODS])
                for kind in range(2):
                    def mv(w):
                        return MODS[:, w * 8:(w + 1) * 8, kind]
                    P.op("dve", STT(DER[:, par, 0, kind, :], mv(1), 1.0, NMIX[:, i, :], ALU.add, ALU.mult),
                         reads=[bMODS, bsmall], writes=[bDER[par]])
                    P.op("dve", STT(DER[:, par, 3, kind, :], mv(4), 1.0, NFFN[:, i, :], ALU.add, ALU.mult),
                         reads=[bMODS, bsmall], writes=[bDER[par]])
                    for dst, w in ((1, 0), (2, 2), (4, 3), (5, 5)):
                        P.op("dve", CP(DER[:, par, dst, kind, :], mv(w)), reads=[bMODS], writes=[bDER[par]])
            dmas = [(lambda mc=mc: piece_dma(mc)) for mc in range(48)]
            mms = [(lambda mc=mc: piece_mm(mc)) for mc in range(48)]
            return dmas, mms, fin

        def emit_mods(i):
            dmas, mms, fin = mods_items(i)
            for a_, b_ in zip(dmas, mms):
                a_()
                b_()
            fin()

        def der(i, which, kind, k):
            return DER[:, i % 2, which, kind, k:k + 1]

        def emit_norm(xt, ti, layer, wm, wsft):
            tok0, T, kind = TILES256[ti]
            ps, bps = psbank(0, 6)
            for k in range(8):
                b = nxt("tb", 2)
                P.op("act", ACTF(TMPB[:, b, 0:T], xt[:, k, :], AF.Square), reads=[bXS[ti % 2]], writes=[bTB[b]])
                P.op("pe", MM(ps[:, 0:T], ones[:], TMPB[:, b, 0:T], k == 0, k == 7), reads=[bTB[b], bconst], writes=[bps])
            a = nxt("ta", 2)
            P.op("act", ACTF(TMPA[:, a, 0:T], ps[:, 0:T], AF.Ln, bias=EPS, scale=1.0 / D), reads=[bps], writes=[bTA[a]])
            P.op("act", ACTF(TMPA[:, a, 0:T], TMPA[:, a, 0:T], AF.Exp, scale=-0.5), reads=[bTA[a]], writes=[bTA[a]])
            for k in range(8):
                pt_, bpt_ = psbank(0, 6)
                P.op("dve", TT(pt_[:, 0:T], xt[:, k, :], TMPA[:, a, 0:T], ALU.mult),
                     reads=[bXS[ti % 2], bTA[a]], writes=[bpt_])
                P.op("act", ACTF(HT[:, k, tok0:tok0 + T], pt_[:, 0:T], AF.Identity,
                                 bias=der(layer, wsft, kind, k), scale=der(layer, wm, kind, k)),
                     reads=[bpt_, bDER[layer % 2]], writes=[bHT[ti]])

        def xslot(ti):
            return XS[:, ti % 2, 0:2048].rearrange("p (k t) -> p k t", k=8)

        def xsrc(X, tok0, T):
            return X.rearrange("(k p) t -> p k t", p=128)[:, :, tok0:tok0 + T]

        def HTbufs(tok0, T):
            return [bHT[i] for i, (t0, tt, _) in enumerate(TILES256) if t0 < tok0 + T and tok0 < t0 + tt]

        def emit_na(i, j):
            def views(c):
                par = c % 2
                QK = LN[:, par, :].bitcast(BF16)[:, 0:2 * NTOK].rearrange("p (a t) -> p a t", a=2)
                VL = LN[:, 2 + par, :].bitcast(BF16)
                VP = VL[:, 0:34 * 130].rearrange("p (g h e) -> p g h e", g=34, h=2)
                WT = LN[:, 2 + par, 2210:2210 + 1920].rearrange("p (h r q) -> p h r q", h=2, r=15)
                YC = LN[:, 4, :].bitcast(BF16)[:, par * NTOK:(par + 1) * NTOK]
                wq = WB[:, par, 0:1024].rearrange("p (k n) -> p k n", k=8)
                wk = WB[:, par, 1024:2048].rearrange("p (k n) -> p k n", k=8)
                wv = WB[:, par, 2048:3072].rearrange("p (k n) -> p k n", k=8)
                return par, QK, VP, WT, YC, wq, wk, wv

            def load(c):
                par, QK, VP, WT, YC, wq, wk, wv = views(c)
                bV = bL[2 + par]
                for wi, dst in enumerate((wq, wk, wv)):
                    load_w(wview(w_qkv[j], wi * D + c * 128, 128), dst, [bWB[par]], 1024)
                fns = []
                for hh in range(2):
                    h = 2 * c + hh
                    base = ((j * 16 + h) * 15) * 127
                    fns.append(DMA(WT[0:64, hh, :, :], bass.AP(tensor=rpbp.tensor, offset=base, ap=[[1, 64], [127, 15], [1, 64]])))
                    fns.append(DMA(WT[64:128, hh, 0:14, :], bass.AP(tensor=rpbp.tensor, offset=base + 127, ap=[[1, 64], [127, 14], [1, 64]])))
                P.dma("sp", fns, writes=[bV])
                P.op("pool", MSET(WT[64:128, :, 14:15, :], 0.0), writes=[bV])
                P.op("dve", TT(WT[:].rearrange("p h r q -> p (h r) q"), WT[:].rearrange("p h r q -> p (h r) q"),
                               cm[:].unsqueeze(1).to_broadcast([128, 30, 64]), ALU.add), reads=[bV, bconst], writes=[bV])
                P.op("pool", MSET(VP[:, :, :, 64:65], 1.0), writes=[bV])

            def proj_items(c, lo, hi, pipelined):
                par, QK, VP, WT, YC, wq, wk, wv = views(c)
                bQK = bL[par]
                bV = bL[2 + par]
                p1s, p2s, vs = [], [], []
                for (tok0, T, kind) in TILES512:
                    for qi, wmat in enumerate((wq, wk)):
                        stt_ = {}

                        def p1(tok0=tok0, T=T, wmat=wmat, stt_=stt_):
                            ps, bps = psbank(lo, hi)
                            P.op("pe", [MM(ps[:, 0:T], wmat[:, k, :], HT[:, k, tok0:tok0 + T], k == 0, k == 7) for k in range(8)],
                                 reads=[bWB[par]] + HTbufs(tok0, T), writes=[bps])
                            b = nxt("tb", 2)
                            P.op("act", ACTF(TMPB[:, b, 0:T], ps[:, 0:T], AF.Square), reads=[bps], writes=[bTB[b]])
                            stt_["v"] = (ps, bps, b)

                        def p2(tok0=tok0, T=T, qi=qi, stt_=stt_):
                            ps, bps, b = stt_["v"]
                            ps2, bps2 = psbank(lo, hi)
                            if bps2 is bps:
                                ps2, bps2 = psbank(lo, hi)
                            P.op("pe", MM(ps2[:, 0:T], bones[:], TMPB[:, b, 0:T]), reads=[bTB[b], bconst], writes=[bps2])
                            a = nxt("ta", 2)
                            P.op("act", ACTF(TMPA[:, a, 0:T], ps2[:, 0:T], AF.Ln, bias=EPS, scale=1.0 / 64), reads=[bps2], writes=[bTA[a]])
                            P.op("act", ACTF(TMPA[:, a, 0:T], TMPA[:, a, 0:T], AF.Exp, scale=-0.5), reads=[bTA[a]], writes=[bTA[a]])
                            P.op("dve", STT(QK[:, qi, tok0:tok0 + T], ps[:, 0:T], QKG[:, j, qi:qi + 1], TMPA[:, a, 0:T], ALU.mult, ALU.mult),
                                 reads=[bps, bTA[a], bsmall], writes=[bQK])
                        p1s.append(p1)
                        p2s.append(p2)
                for g0 in range(0, 34, 4):
                    def vg(g0=g0):
                        ng = min(4, 34 - g0)
                        ps, bps = psbank(lo, hi)
                        fl = []
                        for gi in range(ng):
                            g = g0 + gi
                            fl += [MM(ps[:, gi * 128:(gi + 1) * 128], HT[:, k, g * 128:(g + 1) * 128], wv[:, k, :], k == 0, k == 7) for k in range(8)]
                        P.op("pe", fl, reads=[bWB[par]] + HTbufs(g0 * 128, ng * 128), writes=[bps])
                        P.op("act", ACTF(VP[:, g0:g0 + ng, :, 0:64], ps[:, 0:ng * 128].rearrange("p (g h e) -> p g h e", g=ng, h=2), AF.Copy),
                             reads=[bps], writes=[bV])
                    vs.append(vg)
                items = []
                if pipelined:
                    for n in range(len(p1s)):
                        items.append(p1s[n])
                        if n >= 1:
                            items.append(p2s[n - 1])
                    items.append(p2s[-1])
                else:
                    for n in range(len(p1s)):
                        items.append(p1s[n])
                        items.append(p2s[n])
                return items + vs

            bO = [Buf("O0"), Buf("O1")]

            def fence():
                P.op("dve", CP(RC[:, 0, 0:1], RC[:, 0, 0:1]), reads=[bRC[0]], writes=[bO[0], bO[1], bPS[4], bRC[0]])

            load(0)
            for it in proj_items(0, 0, 6, True):
                it()
            fence()
            for c in range(8):
                nxt_items = []
                if c + 1 < 8:
                    load(c + 1)
                    nxt_items = proj_items(c + 1, 5, 7, False)
                par, QK, VP, WT, YC, wq, wk, wv = views(c)
                bQK = bL[par]
                bV = bL[2 + par]
                bYC = bL[4]
                pipe = Pipe()
                for rp in [0, 1, 2, 3, 32, 33] + list(range(4, 32)):
                    if rp == 4:
                        P.op("pool", MSET(WT[:, :, 11, :], -30000.0), writes=[bV])
                        P.op("pool", MSET(WT[0:64, :, 2, :], -30000.0), writes=[bV])
                        P.op("pool", MSET(WT[64:128, :, 10, :], -30000.0), writes=[bV])
                    oi = nxt("oi", 2)
                    po, bpo = PSB[4][:, oi * 256:(oi + 1) * 256], bO[oi]
                    qtok = rp * 128
                    for hh in range(2):
                        hs = slice(hh * 64, hh * 64 + 64)
                        si = nxt("ss", 2)
                        psS = PSALL[:, si * 1024:(si + 1) * 1024]
                        bS = [bPS[2 * si], bPS[2 * si + 1]]
                        fl = []
                        rsl = []
                        if rp >= 2:
                            m_ = rp - 2
                            rsl = [min(max(2 * m_ + a - 4, 0), 56) for a in range(2)]
                            j0 = rsl[0] // 2
                            assert rsl[1] // 2 == j0
                            nblk = 5 if (rsl[0] % 2 or rsl[1] % 2) else 4
                            i0 = 2 * j0 - 2 * m_ + 7
                            for b in range(nblk):
                                ktok = NCTX + (2 * (j0 + b)) * 64
                                fl.append(MM(psS[:, b * 128:(b + 1) * 128], QK[hs, 1, ktok:ktok + 128], QK[hs, 0, qtok:qtok + 128]))
                        else:
                            nblk = 0
                        nl = nblk * 128
                        for cb in range(2):
                            fl.append(MM(psS[:, nl + cb * 128:nl + (cb + 1) * 128], QK[hs, 1, cb * 128:(cb + 1) * 128], QK[hs, 0, qtok:qtok + 128]))
                        P.op("pe", fl, reads=[bQK], writes=bS)
                        ncol = nl + 256
                        if nblk:
                            sv = psS[:, 0:nl].rearrange("p (b a q) -> p b a q", b=nblk, a=2)
                            w0 = WT[:, hh, i0, :]
                            wap = bass.AP(tensor=w0.tensor, offset=w0.offset + 63, ap=[list(w0.ap[0]), [128, nblk], [-64, 2], [-1, 64]])
                            P.op("dve", TT(sv, sv, wap, ALU.add), reads=bS + [bV], writes=bS)
                        pi = nxt("pt", 2)
                        P.op("act", ACTF(PT[:, pi, 0:ncol], psS[:, 0:ncol], AF.Exp), reads=bS, writes=[bPT[pi]])
                        if nblk == 5:
                            assert 4 <= rp < 32 and i0 == 3
                        def stB(po=po, bpo=bpo, hh=hh, nblk=nblk, nl=nl, pi=pi, j0=(j0 if nblk else 0)):
                            fl = []
                            oreg = po[:, hh * 128:hh * 128 + 65]
                            for b in range(nblk):
                                fl.append(MM(oreg, PT[:, pi, b * 128:(b + 1) * 128], VP[:, 2 + j0 + b, hh, :], b == 0, False))
                            for cb in range(2):
                                fl.append(MM(oreg, PT[:, pi, nl + cb * 128:nl + (cb + 1) * 128], VP[:, cb, hh, :], (nblk + cb) == 0, cb == 1))
                            P.op("pe", fl, reads=[bPT[pi], bV], writes=[bpo])
                        pipe.tick()
                        pipe.defer(1, stB)
                        if nxt_items:
                            nxt_items.pop(0)()
                    def stC(po=po, bpo=bpo, rp=rp):
                        ai = rp % 2
                        pov = po[:, 0:256].rearrange("p (h e) -> p h e", h=2)
                        P.op("dve", RCP(RC[:, ai, :].unsqueeze(2), pov[:, :, 64:65]), reads=[bpo], writes=[bRC[ai]])
                        P.op("dve", TT(AT[:, ai, :].rearrange("p (h e) -> p h e", h=2), pov[:, :, 0:64],
                                       RC[:, ai, :].unsqueeze(2).to_broadcast([128, 2, 64]), ALU.mult),
                             reads=[bpo, bRC[ai]], writes=[bAT[ai]])

                    def stD(rp=rp):
                        ai = rp % 2
                        tr = nxt("tr", 8)
                        P.op("pe", lambda e, tr=tr, ai=ai: e.transpose(PTR[:, tr * 128:(tr + 1) * 128], AT[:, ai, :], ident[:]),
                             reads=[bAT[ai], bconst], writes=[bPTR[tr]])
                        P.op("act", ACTF(YC[:, rp * 128:(rp + 1) * 128], PTR[:, tr * 128:(tr + 1) * 128], AF.Copy), reads=[bPTR[tr]], writes=[bYC])
                    pipe.defer(1, stC)
                    pipe.defer(2, stD)
                pipe.flush()
                for it in nxt_items:
                    it()
                P.dma("sp", DMA(YT[c], YC), reads=[bYC], writes=[bYT], waw=False, semgrp="stYC%d" % par)
            fence()

        def emit_lru(i, j):
            R = LN[:, 0, :]
            XR = LN[:, 1, :]
            OM = LN[:, 2, :]
            IX = LN[:, 3, :]
            HS = LN[:, 4, :]
            XSb = XS[:].rearrange("p a b -> p (a b)").bitcast(BF16)
            GG = XSb[:, 0:NTOK]
            XRb = XSb[:, NTOK:2 * NTOK]
            bGG, bXRb = bXS[0], bXS[1]
            for li in (0, 1):
                for (c0, c1) in ((0, 2), (258, 260), (4356, 4358)):
                    P.op("pool", MSET(LN[:, li, c0:c1], 0.0), writes=[bL[li]])
            def lviews(c):
                par = c % 2
                wg = WB[:, par, 0:1024].rearrange("p (k n) -> p k n", k=8)
                wr = WB[:, par, 1024:2048].rearrange("p (k n) -> p k n", k=8)
                bdw = WB[:, par, 2048:2560].rearrange("p (d g n) -> p d g n", d=2, g=2)
                return par, wg, wr, bdw

            def lload(c):
                par, wg, wr, bdw = lviews(c)
                load_w(wview(w_lin[j], c * 128, 128), wg, [bWB[par]], 1024)
                load_w(wview(w_lin[j], D + c * 128, 128), wr, [bWB[par]], 1024)
                load_w(lbd[:, j, :, :, c, :], bdw, [bWB[par]], 512)

            lload(0)
            for c in range(8):
                if c + 1 < 8:
                    lload(c + 1)
                par, wg, wr, bdw = lviews(c)
                for (tok0, T, kind) in TILES512:
                    ps, bps = psbank(0, 6)
                    P.op("pe", [MM(ps[:, 0:T], wr[:, k, :], HT[:, k, tok0:tok0 + T], k == 0, k == 7) for k in range(8)],
                         reads=[bWB[par]] + HTbufs(tok0, T), writes=[bps])
                    P.op("act", ACTF(R[:, colp(tok0):colp(tok0) + T], ps[:, 0:T], AF.Copy), reads=[bps], writes=[bL[0]])
                    ps, bps = psbank(0, 6)
                    P.op("pe", [MM(ps[:, 0:T], wg[:, k, :], HT[:, k, tok0:tok0 + T], k == 0, k == 7) for k in range(8)],
                         reads=[bWB[par]] + HTbufs(tok0, T), writes=[bps])
                    P.op("act", ACTF(GG[:, tok0:tok0 + T], ps[:, 0:T], AF.Gelu_apprx_tanh), reads=[bps], writes=[bGG])
                n = LW - 4
                P.op("dve", TS(XR[:, 2:2 + n], R[:, 0:n], LCW[:, j, 0, c:c + 1], LCB[:, j, c:c + 1], ALU.mult, ALU.add),
                     reads=[bL[0], bsmall], writes=[bL[1]])
                for tap in (1, 2, 3):
                    P.op("dve", STT(XR[:, 2:2 + n], R[:, tap:tap + n], LCW[:, j, tap, c:c + 1], XR[:, 2:2 + n], ALU.mult, ALU.add),
                         reads=[bL[0], bL[1], bsmall], writes=[bL[1]])
                P.op("pool", CP(XRb[:, 0:NCTX], XR[:, 2:2 + NCTX]), reads=[bL[1]], writes=[bXRb])
                P.op("pool", CP(XRb[:, NCTX:NTOK], XR[:, 260:260 + 4096]), reads=[bL[1]], writes=[bXRb])
                A = R
                for d in range(2):
                    for (tok0, T, kind) in TILES512:
                        cp0 = colp(tok0)
                        psr, bpr = psbank(0, 6)
                        P.op("pe", MM(psr[:, 0:T], bdw[:, d, 0, :], XRb[:, tok0:tok0 + T]), reads=[bWB[par], bXRb], writes=[bpr])
                        psi, bpi = psbank(0, 6)
                        P.op("pe", MM(psi[:, 0:T], bdw[:, d, 1, :], XRb[:, tok0:tok0 + T]), reads=[bWB[par], bXRb], writes=[bpi])
                        a = nxt("ta", 2)
                        P.op("act", ACTF(TMPA[:, a, 0:T], psr[:, 0:T], AF.Tanh, bias=LGBH[:, j, d, 0, c:c + 1], scale=0.5),
                             reads=[bpr, bsmall], writes=[bTA[a]])
                        P.op("act", ACTF(A[:, cp0:cp0 + T], TMPA[:, a, 0:T], AF.Exp, bias=LC8[:, j, d, c, 0:1], scale=LC8[:, j, d, c, 0:1]),
                             reads=[bTA[a], bsmall], writes=[bL[0]])
                        P.op("act", ACTF(OM[:, cp0:cp0 + T], TMPA[:, a, 0:T], AF.Exp, bias=LC8[:, j, d, c, 1:2], scale=LC8[:, j, d, c, 1:2]),
                             reads=[bTA[a], bsmall], writes=[bL[2]])
                        a = nxt("ta", 2)
                        P.op("act", ACTF(TMPA[:, a, 0:T], psi[:, 0:T], AF.Tanh, bias=LGBH[:, j, d, 1, c:c + 1], scale=0.5),
                             reads=[bpi, bsmall], writes=[bTA[a]])
                        P.op("dve", STT(IX[:, cp0:cp0 + T], TMPA[:, a, 0:T], 1.0, XR[:, cp0:cp0 + T], ALU.add, ALU.mult),
                             reads=[bTA[a], bL[1]], writes=[bL[3]])
                    P.op("act", ACTF(OM[:, 2:LW - 2], OM[:, 2:LW - 2], AF.Sqrt, bias=1.0, scale=-1.0), reads=[bL[2]], writes=[bL[2]])
                    P.op("dve", STT(IX[:, 2:LW - 2], IX[:, 2:LW - 2], 0.5, OM[:, 2:LW - 2], ALU.mult, ALU.mult),
                         reads=[bL[3], bL[2]], writes=[bL[3]])
                    dst = HS if d == 0 else OM
                    bdst = bL[4] if d == 0 else bL[2]
                    cs, ls = slice(2, 258), slice(260, 4356)
                    if d == 0:
                        P.op("dve", SCAN(dst[:, cs], A[:, cs], IX[:, cs], 0.0), reads=[bL[0], bL[3]], writes=[bdst])
                        P.op("dve", SCAN(dst[:, ls], A[:, ls], IX[:, ls], dst[:, 257:258]), reads=[bL[0], bL[3], bdst], writes=[bdst])
                    else:
                        rcs, rls = slice(257, 1, -1), slice(4355, 259, -1)
                        P.op("dve", SCAN(dst[:, rcs], A[:, rcs], IX[:, rcs], 0.0), reads=[bL[0], bL[3]], writes=[bdst])
                        P.op("dve", SCAN(dst[:, rls], A[:, rls], IX[:, rls], dst[:, 2:3]), reads=[bL[0], bL[3], bdst], writes=[bdst])
                for (s0, s1, t0, t1) in ((2, 258, 0, 256), (260, 4356, 256, NTOK)):
                    P.op("dve", TT(HS[:, s0:s1], HS[:, s0:s1], OM[:, s0:s1], ALU.add), reads=[bL[4], bL[2]], writes=[bL[4]])
                    P.op("dve", TT(GG[:, t0:t1], HS[:, s0:s1], GG[:, t0:t1], ALU.mult), reads=[bL[4], bGG], writes=[bGG])
                P.dma("sp", DMA(YT[c], GG), reads=[bGG], writes=[bYT], waw=False, semgrp="stXS0")

        def emit_m3(i, j, is_na):
            Xsrc, bXsrc = (xin, None) if i == 0 else (XA, bXA)
            WO = LN[:, 0, :].bitcast(BF16)[:, 0:8192].rearrange("p (k n) -> p k n", k=8)
            wsrc = (w_nao if is_na else w_lout)[j]
            for k in range(8):
                load_w(wsrc[k * 128:(k + 1) * 128, :], WO[:, k, :], [bL[0]], 1024)
            YL = LN[:, 2, :].bitcast(BF16)
            for ti, (tok0, T, kind) in enumerate(TILES256):
                xt = xslot(ti)
                yt = YL[:, (ti % 2) * 2048:(ti % 2) * 2048 + 2048].rearrange("p (k t) -> p k t", k=8)
                P.dma("sp", DMA(xt, xsrc(Xsrc, tok0, T)), reads=([bXsrc] if bXsrc else []), writes=[bXS[ti % 2]])
                P.dma("sp", DMA(yt, YT[:, :, tok0:tok0 + T].rearrange("k p t -> p k t")), reads=[bYT], writes=[bL[2]])
                for m in range(8):
                    ps, bps = psbank(0, 6)
                    P.op("pe", [MM(ps[:, 0:T], WO[:, k, m * 128:(m + 1) * 128], yt[:, k, :], k == 0, k == 7) for k in range(8)],
                         reads=[bL[0], bL[2]], writes=[bps])
                    P.op("dve", STT(xt[:, m, :], ps[:, 0:T], der(i, 2, kind, m), xt[:, m, :], ALU.mult, ALU.add),
                         reads=[bps, bDER[i % 2], bXS[ti % 2]], writes=[bXS[ti % 2]])
                P.dma("sp", DMA(xsrc(XB, tok0, T), xt), reads=[bXS[ti % 2]], writes=[bXB], waw=False, semgrp="stXS%d" % (ti % 2))
                emit_norm(xt, ti, i, 3, 4)

        def emit_f1(i, extra=None):
            xd, xm, xfin = extra if extra else ([], [], None)
            pend = []
            P.dma("sp", [DMA(FCW[:], fcw[:, i]), DMA(FCB[:], fcb[:, i])], writes=[bFC])

            def fload(c):
                wu_ = WB[:, c % 2, 0:2048].rearrange("p (w k n) -> p w k n", w=2, k=8)
                load_w(wview(w_up[i], c * 128, 128), wu_[:, 0], [bWB[c % 2]], 1024)
                load_w(wview(w_up[i], 3 * D + c * 128, 128), wu_[:, 1], [bWB[c % 2]], 1024)

            fload(0)
            for c in range(24):
                for f in pend:
                    f()
                pend = []
                if c + 1 < 24:
                    fload(c + 1)
                for _ in range(2):
                    if xd:
                        xd.pop(0)()
                        pend.append(xm.pop(0))
                par = c % 2
                wu = WB[:, par, 0:2048].rearrange("p (w k n) -> p w k n", w=2, k=8)
                accs = ((LN[:, 2 * par, :], bL[2 * par]), (LN[:, 2 * par + 1, :], bL[2 * par + 1]))
                AV, AG = accs[0][0], accs[1][0]
                Gst = LN[:, 4, :].bitcast(BF16)[:, par * NTOK:(par + 1) * NTOK]
                for w, (ACC, bACC) in enumerate(accs):
                    ch = w * 24 + c
                    for c0_ in (2, 260):
                        P.op("act", ACTF(ACC[:, c0_:c0_ + 1], FCB[:, ch:ch + 1], AF.Identity), reads=[bFC], writes=[bACC])
                for (tok0, T, kind) in TILES512:
                    cp0 = colp(tok0)
                    for w, (ACC, bACC) in enumerate(accs):
                        ch = w * 24 + c
                        ps, bps = psbank(0, 6)
                        P.op("pe", [MM(ps[:, 0:T], wu[:, w, k, :], HT[:, k, tok0:tok0 + T], k == 0, k == 7) for k in range(8)],
                             reads=[bWB[par]] + HTbufs(tok0, T), writes=[bps])
                        P.op("act", ACTF(ACC[:, cp0 + 1:cp0 + 1 + T], ps[:, 0:T], AF.Identity, bias=FCB[:, ch:ch + 1], scale=FCW[:, 0, ch:ch + 1]),
                             reads=[bps, bFC], writes=[bACC])
                        P.op("dve", STT(ACC[:, cp0:cp0 + T], ps[:, 0:T], FCW[:, 1, ch:ch + 1], ACC[:, cp0:cp0 + T], ALU.mult, ALU.add),
                             reads=[bps, bFC, bACC], writes=[bACC])
                        P.op("dve", STT(ACC[:, cp0 - 1:cp0 - 1 + T], ps[:, 0:T], FCW[:, 2, ch:ch + 1], ACC[:, cp0 - 1:cp0 - 1 + T], ALU.mult, ALU.add),
                             reads=[bps, bFC, bACC], writes=[bACC])
                P.op("act", ACTF(AG[:, 2:LW - 2], AG[:, 2:LW - 2], AF.Silu), reads=[accs[1][1]], writes=[accs[1][1]])
                for (s0, s1, t0, t1) in ((2, 258, 0, 256), (260, 4356, 256, NTOK)):
                    P.op("pool", TT(Gst[:, t0:t1], AV[:, s0:s1], AG[:, s0:s1], ALU.mult), reads=[accs[0][1], accs[1][1]], writes=[bL[4]])
                P.dma("sp", DMA(GS[c], Gst), reads=[bL[4]], writes=[bGS], waw=False, semgrp="stL4_%d" % par)
            for f in pend:
                f()
            assert not xd
            if xfin:
                xfin()

        def emit_f2(i, last):
            WDn = LN[:, 0:3, :].rearrange("p a b -> p (a b)").bitcast(BF16)[:, 0:24 * 1024].rearrange("p (k n) -> p k n", k=24)
            for k in range(24):
                load_w(w_down[i][k * 128:(k + 1) * 128, :], WDn[:, k, :], [bL[0], bL[1], bL[2]], 1024)
            for ti, (tok0, T, kind) in enumerate(TILES256):
                if last and kind == 1:
                    continue
                xt = xslot(ti)
                gt = LN[:, 3 + ti % 2, :].bitcast(BF16)[:, 0:24 * 256].rearrange("p (k t) -> p k t", k=24)
                P.dma("sp", DMA(xt, xsrc(XB, tok0, T)), reads=[bXB], writes=[bXS[ti % 2]])
                P.dma("sp", DMA(gt, GS[:, :, tok0:tok0 + T].rearrange("k p t -> p k t")), reads=[bGS], writes=[bL[3 + ti % 2]])
                for m in range(8):
                    ps, bps = psbank(0, 6)
                    P.op("pe", [MM(ps[:, 0:T], WDn[:, k, m * 128:(m + 1) * 128], gt[:, k, :], k == 0, k == 23) for k in range(24)],
                         reads=[bL[0], bL[1], bL[2], bL[3 + ti % 2]], writes=[bps])
                    P.op("dve", STT(xt[:, m, :], ps[:, 0:T], der(i, 5, kind, m), xt[:, m, :], ALU.mult, ALU.add),
                         reads=[bps, bDER[i % 2], bXS[ti % 2]], writes=[bXS[ti % 2]])
                if last:
                    P.dma("sp", DMA(xsrc(out, tok0 - NCTX, T), xt), reads=[bXS[ti % 2]], writes=[bOUT], waw=False, semgrp="stXS%d" % (ti % 2))
                else:
                    P.dma("sp", DMA(xsrc(XA, tok0, T), xt), reads=[bXS[ti % 2]], writes=[bXA], waw=False, semgrp="stXS%d" % (ti % 2))
                    emit_norm(xt, ti, i + 1, 0, 1)

        emit_mods(0)
        for ti, (tok0, T, kind) in enumerate(TILES256):
            xt = xslot(ti)
            P.dma("sp", DMA(xt, xsrc(xin, tok0, T)), writes=[bXS[ti % 2]])
            emit_norm(xt, ti, 0, 0, 1)
        for i in range(nlayers):
            j = i // 2
            is_na = (i % 2 == 0)
            if is_na:
                emit_na(i, j)
            else:
                emit_lru(i, j)
            emit_m3(i, j, is_na)
            emit_f1(i, mods_items(i + 1) if i + 1 < nlayers else None)
            emit_f2(i, i == nlayers - 1)
        P.final_wait("sp", [bOUT])
        P.emit()
    return nc


_CACHE = {}


def _prep_shared(inp):
    f = np.float32

    def pcol(v, lead):
        v = np.asarray(v, f)
        v = v.reshape(lead + (8, 128))
        return np.ascontiguousarray(np.moveaxis(v, -1, 0))
    sh = {}
    sh["ada_w"] = np.ascontiguousarray(inp["ada_w"], f)
    sh["ada_b"] = np.ascontiguousarray(np.asarray(inp["ada_b"], f).reshape(DEPTH, 48, 128).transpose(2, 0, 1))
    sh["nmix"] = pcol(inp["norm_mix"], (DEPTH,))
    sh["nffn"] = pcol(inp["norm_ffn"], (DEPTH,))
    sh["w_qkv"] = np.ascontiguousarray(inp["na_w_qkv"], f)
    qg = np.asarray(inp["na_q_gain"], f)
    kg = np.asarray(inp["na_k_gain"], f)
    qkg = np.zeros((128, 2, 2), f)
    for jj in range(2):
        qkg[:, jj, 0] = np.tile(qg[jj], 2)
        qkg[:, jj, 1] = np.tile(kg[jj], 2)
    sh["qkg"] = qkg
    rp = np.zeros((2, 16, 15, 127), f)
    rp[:, :, :, 48:79] = np.asarray(inp["na_rpb"], f)
    sh["rpbp"] = rp
    sh["w_nao"] = np.ascontiguousarray(inp["na_w_out"], f)
    sh["w_lin"] = np.ascontiguousarray(inp["lru_w_in"], f)
    sh["lcw"] = pcol(inp["lru_conv_w"], (2, 4))
    sh["lcb"] = pcol(inp["lru_conv_b"], (2,))
    ga = np.asarray(inp["lru_ga_w"], f)
    gx = np.asarray(inp["lru_gx_w"], f)
    lbd = np.zeros((128, 2, 2, 2, 8, 128), f)
    for g, w in enumerate((ga, gx)):
        for jj in range(2):
            for d in range(2):
                for c in range(8):
                    for hb in range(2):
                        lbd[hb * 64:(hb + 1) * 64, jj, d, g, c, hb * 64:(hb + 1) * 64] = w[jj, d, 2 * c + hb]
    sh["lbd"] = lbd
    gab = pcol(inp["lru_ga_b"], (2, 2))
    gxb = pcol(inp["lru_gx_b"], (2, 2))
    sh["lgb"] = np.ascontiguousarray(np.stack([gab, gxb], axis=3))
    sh["llam"] = pcol(inp["lru_lambda"], (2, 2))
    sh["w_lout"] = np.ascontiguousarray(inp["lru_w_out"], f)
    sh["w_up"] = np.ascontiguousarray(inp["ffn_w_up"], f)
    sh["fcw"] = np.ascontiguousarray(np.asarray(inp["ffn_conv_w"], f).reshape(DEPTH, 3, 48, 128).transpose(3, 0, 1, 2))
    sh["fcb"] = np.ascontiguousarray(np.asarray(inp["ffn_conv_b"], f).reshape(DEPTH, 48, 128).transpose(2, 0, 1))
    sh["w_down"] = np.ascontiguousarray(inp["ffn_w_down"], f)
    bf = ml_dtypes.bfloat16
    sh["c_ident"] = np.eye(128, dtype=f).astype(bf)
    bo = np.zeros((128, 128), f)
    bo[:64, :64] = 1
    bo[64:, 64:] = 1
    sh["c_bones"] = bo.astype(bf)
    sh["c_ones"] = np.ones((128, 128), f).astype(bf)
    kc = np.arange(64)[:, None]
    qc = 63 - np.arange(64)[None, :]
    ws = np.clip(qc - 8, 0, 48)
    ok = (kc >= ws) & (kc < ws + 16)
    cmv = np.where(ok, 0.0, -30000.0).astype(f)
    sh["c_cm"] = np.ascontiguousarray(np.concatenate([cmv, cmv], 0))
    return sh


def kernel(nlayers=DEPTH, **inp):
    if nlayers not in _CACHE:
        _CACHE[nlayers] = build_program(nlayers)
    nc = _CACHE[nlayers]
    sh = _prep_shared(inp)
    x = np.asarray(inp["x"], np.float32)
    ctx = np.asarray(inp["ctx"], np.float32)
    c = np.asarray(inp["c"], np.float32)
    cc = np.asarray(inp["c_ctx"], np.float32)
    in_maps = []
    for b in range(8):
        m = dict(sh)
        m["xin"] = np.ascontiguousarray(np.concatenate([ctx[b], x[b]], 0).T)
        col = np.stack([c[b].reshape(8, 128).T, cc.reshape(8, 128).T], axis=2)
        m["ccol"] = np.ascontiguousarray(col, np.float32)
        in_maps.append(m)
    res = run_bass_kernel_spmd(nc, in_maps, core_ids=list(range(8)))
    outs = [np.ascontiguousarray(res.results[b]["out"].T) for b in range(8)]
    return np.stack(outs, 0).astype(np.float32)
```

```python
import math
from contextlib import ExitStack

import numpy as np
import ml_dtypes
import concourse.bass as bass
import concourse.mybir as mybir
from concourse.bass_utils import run_bass_kernel_spmd

F32 = mybir.dt.float32
BF16 = mybir.dt.bfloat16
ALU = mybir.AluOpType
AF = mybir.ActivationFunctionType

D = 1024
NTOK = 4352
NCTX = 256
DEPTH = 4
EPS = 1e-6
LW = 4358


class Buf:
    __slots__ = ("name", "w", "r", "grp")

    def __init__(self, name, grp=None):
        self.name = name
        self.w = {}
        self.r = {}
        self.grp = grp or name


class Prog:
    ENG = ("pe", "act", "dve", "pool", "sp")

    def __init__(self, nc, stack):
        self.nc = nc
        self.stack = stack
        self.q = {e: [] for e in self.ENG}
        self.sem = {}
        self.cnt = {}
        self.waited = {e: {} for e in self.ENG}
        for e in ("pe", "act", "dve", "pool"):
            self._mksem("E_" + e)

    def _mksem(self, key):
        if key not in self.sem:
            self.sem[key] = self.stack.enter_context(self.nc.semaphore("s_" + key))
            self.cnt[key] = 0
        return key

    def _waits(self, qn, reads, writes, is_dma, waw=True):
        own = None if is_dma else "E_" + qn
        w = {}

        def merge(d, skip_own):
            for k, v in d.items():
                if skip_own and k == own:
                    continue
                if v > w.get(k, 0):
                    w[k] = v
        for b in reads:
            merge(b.w, False)
        for b in writes:
            if waw:
                merge(b.w, True)
            merge(b.r, True)
        out = []
        wd = self.waited[qn]
        for k, v in w.items():
            if v > wd.get(k, 0):
                wd[k] = v
                out.append((k, v))
        return out

    def op(self, qn, fns, reads=(), writes=()):
        if callable(fns):
            fns = [fns]
        waits = self._waits(qn, reads, writes, False)
        key = "E_" + qn
        self.cnt[key] += 1
        val = self.cnt[key]
        self.q[qn].append((waits, fns, (key, 1)))
        for b in reads:
            b.r[key] = val
        for b in writes:
            b.w[key] = val
            b.r = {}

    def dma(self, qn, fns, reads=(), writes=(), waw=True, semgrp=None):
        if callable(fns):
            fns = [fns]
        waits = self._waits(qn, reads, writes, True, waw)
        key = self._mksem("D_" + (semgrp or writes[0].grp))
        self.cnt[key] += 16 * len(fns)
        val = self.cnt[key]
        self.q[qn].append((waits, fns, (key, 16, True)))
        for b in reads:
            b.r[key] = val
        for b in writes:
            b.w[key] = val
            b.r = {}

    def final_wait(self, qn, bufs):
        waits = self._waits(qn, bufs, (), True)
        self.q[qn].append((waits, [], None))

    def emit(self):
        nc = self.nc
        sem = self.sem

        def run(eng, items):
            for waits, fns, sig in items:
                for k, v in waits:
                    eng.wait_ge(sem[k], v)
                n = len(fns)
                for i, fn in enumerate(fns):
                    r = fn(eng)
                    if sig is not None and (len(sig) == 3 or i == n - 1):
                        r.then_inc(sem[sig[0]], sig[1])
        with nc.Block() as block:
            @block.tensor
            def _(e):
                run(e, self.q["pe"])

            @block.scalar
            def _(e):
                run(e, self.q["act"])

            @block.vector
            def _(e):
                run(e, self.q["dve"])

            @block.gpsimd
            def _(e):
                run(e, self.q["pool"])

            @block.sync
            def _(e):
                run(e, self.q["sp"])


def TT(out, in0, in1, op):
    return lambda e: e.tensor_tensor(out=out, in0=in0, in1=in1, op=op)


def TS(out, in0, s1, s2, op0, op1=None):
    if op1 is None:
        return lambda e: e.tensor_scalar(out=out, in0=in0, scalar1=s1, scalar2=None, op0=op0)
    return lambda e: e.tensor_scalar(out=out, in0=in0, scalar1=s1, scalar2=s2, op0=op0, op1=op1)


def STT(out, in0, scalar, in1, op0, op1):
    return lambda e: e.scalar_tensor_tensor(out=out, in0=in0, scalar=scalar, in1=in1, op0=op0, op1=op1)


def ACTF(out, in_, func, bias=0.0, scale=1.0):
    return lambda e: e.activation(out=out, in_=in_, func=func, bias=bias, scale=scale)


def MM(out, lhsT, rhs, start=True, stop=True):
    return lambda e: e.matmul(out, lhsT=lhsT, rhs=rhs, start=start, stop=stop)


def CP(out, in_):
    return lambda e: e.tensor_copy(out=out, in_=in_)


def RCP(out, in_):
    return lambda e: e.reciprocal(out=out, in_=in_)


def MSET(ap, v):
    return lambda e: e.memset(ap, v)


def DMA(out, in_):
    return lambda e: e.dma_start(out=out, in_=in_)


def SCAN(out, d0, d1, init):
    return lambda e: e.tensor_tensor_scan(out=out, data0=d0, data1=d1, initial=init, op0=ALU.mult, op1=ALU.add)


class Pipe:
    def __init__(self):
        self.items = []
        self.t = 0
        self.n = 0

    def defer(self, delay, fn):
        self.items.append((self.t + delay, self.n, fn))
        self.n += 1

    def tick(self):
        self.t += 1
        ready = sorted([it for it in self.items if it[0] <= self.t])
        self.items = [it for it in self.items if it[0] > self.t]
        for _, _, fn in ready:
            fn()

    def flush(self):
        while self.items:
            self.tick()


TILES256 = [(0, 256, 1)] + [(256 + 256 * j, 256, 0) for j in range(16)]
TILES512 = [(0, 256, 1)] + [(256 + 512 * j, 512, 0) for j in range(8)]


def colp(tok):
    return tok + 2 if tok < NCTX else tok + 4


def build_program(nlayers=DEPTH):
    nc = bass.Bass("TRN2", target_bir_lowering=False)

    def din(name, shape, dt=F32):
        return nc.dram_tensor(name, list(shape), dt, kind="ExternalInput").ap()

    xin = din("xin", [D, NTOK])
    ccol = din("ccol", [128, 8, 2])
    ada_w = din("ada_w", [DEPTH, D, 6 * D])
    ada_b = din("ada_b", [128, DEPTH, 48])
    nmix = din("nmix", [128, DEPTH, 8])
    nffn = din("nffn", [128, DEPTH, 8])
    w_qkv = din("w_qkv", [2, D, 3 * D])
    qkg = din("qkg", [128, 2, 2])
    rpbp = din("rpbp", [2, 16, 15, 127])
    w_nao = din("w_nao", [2, D, D])
    w_lin = din("w_lin", [2, D, 2 * D])
    lcw = din("lcw", [128, 2, 4, 8])
    lcb = din("lcb", [128, 2, 8])
    lbd = din("lbd", [128, 2, 2, 2, 8, 128])
    lgb = din("lgb", [128, 2, 2, 2, 8])
    llam = din("llam", [128, 2, 2, 8])
    w_lout = din("w_lout", [2, D, D])
    w_up = din("w_up", [DEPTH, D, 6 * D])
    fcw = din("fcw", [128, DEPTH, 3, 48])
    fcb = din("fcb", [128, DEPTH, 48])
    w_down = din("w_down", [DEPTH, 3 * D, D])
    c_ident = din("c_ident", [128, 128], BF16)
    c_bones = din("c_bones", [128, 128], BF16)
    c_ones = din("c_ones", [128, 128], BF16)
    c_cm = din("c_cm", [128, 64])
    out = nc.dram_tensor("out", [D, NTOK - NCTX], F32, kind="ExternalOutput").ap()
    XA = nc.dram_tensor("XA", [D, NTOK], F32, kind="Internal").ap()
    XB = nc.dram_tensor("XB", [D, NTOK], F32, kind="Internal").ap()
    YT = nc.dram_tensor("YT", [8, 128, NTOK], BF16, kind="Internal").ap()
    GS = nc.dram_tensor("GS", [24, 128, NTOK], BF16, kind="Internal").ap()
    bXA, bXB, bYT, bGS, bOUT = Buf("XA"), Buf("XB"), Buf("YT"), Buf("GS"), Buf("OUT")

    with ExitStack() as st:
        P = Prog(nc, st)

        def sb(name, shape, dt=F32):
            return st.enter_context(nc.sbuf_tensor(name, list(shape), dt))

        HT = sb("HT", [128, 8, NTOK], BF16)
        bHT = [Buf("HT%d" % i) for i in range(len(TILES256))]
        LN = sb("LN", [128, 5, LW], F32)
        bL = [Buf("L%d" % i) for i in range(5)]
        XS = sb("XS", [128, 2, 2176], F32)
        bXS = [Buf("XS0"), Buf("XS1")]
        WS = sb("WS", [128, 2, 1024], F32)
        bWS = [Buf("WS%d" % i) for i in range(2)]
        WB = sb("WB", [128, 2, 3072], BF16)
        bWB = [Buf("WB0"), Buf("WB1")]
        ident = sb("ident", [128, 128], BF16)
        bones = sb("bones", [128, 128], BF16)
        ones = sb("ones", [128, 128], BF16)
        cm = sb("cm", [128, 64], F32)
        bconst = Buf("const")
        MODS = sb("MODS", [128, 48, 2], F32)
        bMODS = Buf("MODS")
        DER = sb("DER", [128, 2, 6, 2, 8], F32)
        bDER = [Buf("DER0"), Buf("DER1")]
        S2 = sb("S2", [128, 8, 2], F32)
        bS2 = Buf("S2")
        ADAB = sb("ADAB", [128, DEPTH, 48], F32)
        NMIX = sb("NMIX", [128, DEPTH, 8], F32)
        NFFN = sb("NFFN", [128, DEPTH, 8], F32)
        QKG = sb("QKG", [128, 2, 2], F32)
        LCW = sb("LCW", [128, 2, 4, 8], F32)
        LCB = sb("LCB", [128, 2, 8], F32)
        LGB = sb("LGB", [128, 2, 2, 2, 8], F32)
        LLAM = sb("LLAM", [128, 2, 2, 8], F32)
        LC8 = sb("LC8", [128, 2, 2, 8, 2], F32)
        LGBH = sb("LGBH", [128, 2, 2, 2, 8], F32)
        FCW = sb("FCW", [128, 3, 48], F32)
        FCB = sb("FCB", [128, 48], F32)
        bFC = Buf("FC")
        bsmall = Buf("small")
        TMPA = sb("TMPA", [128, 2, 512], F32)
        bTA = [Buf("TA0"), Buf("TA1")]
        TMPB = sb("TMPB", [128, 2, 512], BF16)
        bTB = [Buf("TB0"), Buf("TB1")]
        PT = sb("PT", [128, 2, 896], BF16)
        bPT = [Buf("PT0"), Buf("PT1")]
        AT = sb("AT", [128, 2, 128], BF16)
        bAT = [Buf("AT0"), Buf("AT1")]
        RC = sb("RC", [128, 2, 2], F32)
        bRC = [Buf("RC0"), Buf("RC1")]
        PSALL = st.enter_context(nc.psum_tensor("psall", [128, 7 * 512], F32))
        PSB = [PSALL[:, i * 512:(i + 1) * 512] for i in range(7)]
        bPS = [Buf("ps%d" % i) for i in range(7)]
        PTR = st.enter_context(nc.psum_tensor("ptr", [128, 1024], BF16))
        bPTR = [Buf("ptr%d" % i) for i in range(8)]
        rr = {"ps": 0, "ws": 0, "ta": 0, "tb": 0, "tr": 0, "pt": 0, "ss": 0}

        def nxt(key, n):
            v = rr[key]
            rr[key] = (v + 1) % n
            return v

        def psbank(lo=0, hi=7):
            key = "ps%d_%d" % (lo, hi)
            if key not in rr:
                rr[key] = 0
            i = lo + nxt(key, hi - lo)
            return PSB[i], bPS[i]

        def ld(dst, src, b):
            P.dma("sp", DMA(dst, src), writes=[b])
        ld(ident[:], c_ident, bconst)
        ld(bones[:], c_bones, bconst)
        ld(ones[:], c_ones, bconst)
        ld(cm[:], c_cm, bconst)
        for dst, src in ((ADAB, ada_b), (NMIX, nmix), (NFFN, nffn), (QKG, qkg), (LCW, lcw), (LCB, lcb),
                         (LGB, lgb), (LLAM, llam)):
            ld(dst[:], src, bsmall)
        ld(S2[:], ccol, bS2)
        P.op("act", ACTF(S2[:], S2[:], AF.Silu), reads=[bS2], writes=[bS2])
        P.op("dve", TS(QKG[:, :, 0:1], QKG[:, :, 0:1], 0.125, None, ALU.mult), reads=[bsmall], writes=[bsmall])
        P.op("act", ACTF(LLAM[:], LLAM[:], AF.Exp, scale=-1.0), reads=[bsmall], writes=[bsmall])
        P.op("act", ACTF(LLAM[:], LLAM[:], AF.Ln, bias=1.0), reads=[bsmall], writes=[bsmall])
        P.op("dve", TS(LC8[:, :, :, :, 0], LLAM[:], -4.0, None, ALU.mult), reads=[bsmall], writes=[bsmall])
        P.op("dve", TS(LC8[:, :, :, :, 1], LLAM[:], -8.0, None, ALU.mult), reads=[bsmall], writes=[bsmall])
        P.op("dve", TS(LGBH[:], LGB[:], 0.5, None, ALU.mult), reads=[bsmall], writes=[bsmall])

        def load_w(src_ap, dst_ap, dst_bufs, n):
            s = nxt("ws", 2)
            shp = src_ap.shape
            if len(shp) == 3:
                stage = WS[:, s, 0:n].rearrange("p (a b) -> p a b", a=shp[1])
            else:
                stage = WS[:, s, 0:n]
            P.dma("sp", DMA(stage, src_ap), writes=[bWS[s]])
            P.op("pool", CP(dst_ap, stage), reads=[bWS[s]], writes=dst_bufs)

        def wview(w2d, c0, ncols):
            return w2d.rearrange("(k p) n -> p k n", p=128)[:, :, c0:c0 + ncols]

        def mods_items(i):
            par = i % 2
            ps, bps = PSB[6], bPS[6]
            items = []

            slot = {}

            def piece_dma(mc):
                s_ = nxt("ws", 2)
                slot[mc] = s_
                stage = WS[:, s_, :].rearrange("p (k n) -> p k n", k=8)
                P.dma("sp", DMA(stage, wview(ada_w[i], mc * 128, 128)), writes=[bWS[s_]])

            def piece_mm(mc):
                s_ = slot[mc]
                stage = WS[:, s_, :].rearrange("p (k n) -> p k n", k=8)
                P.op("pe", [MM(ps[:, mc * 2:mc * 2 + 2], stage[:, k, :], S2[:, k, :], k == 0, k == 7) for k in range(8)],
                     reads=[bWS[s_], bS2], writes=[bps])

            def fin():
                psv = ps[:, 0:96].rearrange("p (m t) -> p m t", t=2)
                P.op("dve", TT(MODS[:], psv, ADAB[:, i, :].unsqueeze(2).to_broadcast([128, 48, 2]), ALU.add),
                     reads=[bps, bsmall], writes=[bMODS])
                for kind in range(2):
                    def mv(w):
                        return MODS[:, w * 8:(w + 1) * 8, kind]
                    P.op("dve", STT(DER[:, par, 0, kind, :], mv(1), 1.0, NMIX[:, i, :], ALU.add, ALU.mult),
                         reads=[bMODS, bsmall], writes=[bDER[par]])
                    P.op("dve", STT(DER[:, par, 3, kind, :], mv(4), 1.0, NFFN[:, i, :], ALU.add, ALU.mult),
                         reads=[bMODS, bsmall], writes=[bDER[par]])
                    for dst, w in ((1, 0), (2, 2), (4, 3), (5, 5)):
                        P.op("dve", CP(DER[:, par, dst, kind, :], mv(w)), reads=[bMODS], writes=[bDER[par]])
            dmas = [(lambda mc=mc: piece_dma(mc)) for mc in range(48)]
            mms = [(lambda mc=mc: piece_mm(mc)) for mc in range(48)]
            return dmas, mms, fin

        def emit_mods(i):
            dmas, mms, fin = mods_items(i)
            for a_, b_ in zip(dmas, mms):
                a_()
                b_()
            fin()

        def der(i, which, kind, k):
            return DER[:, i % 2, which, kind, k:k + 1]

        def emit_norm(xt, ti, layer, wm, wsft):
            tok0, T, kind = TILES256[ti]
            ps, bps = psbank(0, 6)
            for k in range(8):
                b = nxt("tb", 2)
                P.op("act", ACTF(TMPB[:, b, 0:T], xt[:, k, :], AF.Square), reads=[bXS[ti % 2]], writes=[bTB[b]])
                P.op("pe", MM(ps[:, 0:T], ones[:], TMPB[:, b, 0:T], k == 0, k == 7), reads=[bTB[b], bconst], writes=[bps])
            a = nxt("ta", 2)
            P.op("act", ACTF(TMPA[:, a, 0:T], ps[:, 0:T], AF.Ln, bias=EPS, scale=1.0 / D), reads=[bps], writes=[bTA[a]])
            P.op("act", ACTF(TMPA[:, a, 0:T], TMPA[:, a, 0:T], AF.Exp, scale=-0.5), reads=[bTA[a]], writes=[bTA[a]])
            for k in range(8):
                pt_, bpt_ = psbank(0, 6)
                P.op("dve", TT(pt_[:, 0:T], xt[:, k, :], TMPA[:, a, 0:T], ALU.mult),
                     reads=[bXS[ti % 2], bTA[a]], writes=[bpt_])
                P.op("act", ACTF(HT[:, k, tok0:tok0 + T], pt_[:, 0:T], AF.Identity,
                                 bias=der(layer, wsft, kind, k), scale=der(layer, wm, kind, k)),
                     reads=[bpt_, bDER[layer % 2]], writes=[bHT[ti]])

        def xslot(ti):
            return XS[:, ti % 2, 0:2048].rearrange("p (k t) -> p k t", k=8)

        def xsrc(X, tok0, T):
            return X.rearrange("(k p) t -> p k t", p=128)[:, :, tok0:tok0 + T]

        def HTbufs(tok0, T):
            return [bHT[i] for i, (t0, tt, _) in enumerate(TILES256) if t0 < tok0 + T and tok0 < t0 + tt]

        def emit_na(i, j):
            def views(c):
                par = c % 2
                QK = LN[:, par, :].bitcast(BF16)[:, 0:2 * NTOK].rearrange("p (a t) -> p a t", a=2)
                VL = LN[:, 2 + par, :].bitcast(BF16)
                VP = VL[:, 0:34 * 130].rearrange("p (g h e) -> p g h e", g=34, h=2)
                WT = LN[:, 2 + par, 2210:2210 + 1920].rearrange("p (h r q) -> p h r q", h=2, r=15)
                YC = LN[:, 4, :].bitcast(BF16)[:, par * NTOK:(par + 1) * NTOK]
                wq = WB[:, par, 0:1024].rearrange("p (k n) -> p k n", k=8)
                wk = WB[:, par, 1024:2048].rearrange("p (k n) -> p k n", k=8)
                wv = WB[:, par, 2048:3072].rearrange("p (k n) -> p k n", k=8)
                return par, QK, VP, WT, YC, wq, wk, wv

            def load(c):
                par, QK, VP, WT, YC, wq, wk, wv = views(c)
                bV = bL[2 + par]
                for wi, dst in enumerate((wq, wk, wv)):
                    load_w(wview(w_qkv[j], wi * D + c * 128, 128), dst, [bWB[par]], 1024)
                fns = []
                for hh in range(2):
                    h = 2 * c + hh
                    base = ((j * 16 + h) * 15) * 127
                    fns.append(DMA(WT[0:64, hh, :, :], bass.AP(tensor=rpbp.tensor, offset=base, ap=[[1, 64], [127, 15], [1, 64]])))
                    fns.append(DMA(WT[64:128, hh, 0:14, :], bass.AP(tensor=rpbp.tensor, offset=base + 127, ap=[[1, 64], [127, 14], [1, 64]])))
                P.dma("sp", fns, writes=[bV])
                P.op("pool", MSET(WT[64:128, :, 14:15, :], 0.0), writes=[bV])
                P.op("dve", TT(WT[:].rearrange("p h r q -> p (h r) q"), WT[:].rearrange("p h r q -> p (h r) q"),
                               cm[:].unsqueeze(1).to_broadcast([128, 30, 64]), ALU.add), reads=[bV, bconst], writes=[bV])
                P.op("pool", MSET(VP[:, :, :, 64:65], 1.0), writes=[bV])

            load(0)
            for c in range(8):
                if c + 1 < 8:
                    load(c + 1)
                par, QK, VP, WT, YC, wq, wk, wv = views(c)
                bQK = bL[par]
                bV = bL[2 + par]
                bYC = bL[4]
                pipe = Pipe()
                for (tok0, T, kind) in TILES512:
                    for qi, wmat in enumerate((wq, wk)):
                        ps, bps = psbank(0, 6)
                        P.op("pe", [MM(ps[:, 0:T], wmat[:, k, :], HT[:, k, tok0:tok0 + T], k == 0, k == 7) for k in range(8)],
                             reads=[bWB[par]] + HTbufs(tok0, T), writes=[bps])
                        b = nxt("tb", 2)
                        P.op("act", ACTF(TMPB[:, b, 0:T], ps[:, 0:T], AF.Square), reads=[bps], writes=[bTB[b]])

                        def p2(ps=ps, bps=bps, b=b, tok0=tok0, T=T, qi=qi):
                            ps2, bps2 = psbank(0, 6)
                            if bps2 is bps:
                                ps2, bps2 = psbank(0, 6)
                            P.op("pe", MM(ps2[:, 0:T], bones[:], TMPB[:, b, 0:T]), reads=[bTB[b], bconst], writes=[bps2])
                            a = nxt("ta", 2)
                            P.op("act", ACTF(TMPA[:, a, 0:T], ps2[:, 0:T], AF.Ln, bias=EPS, scale=1.0 / 64), reads=[bps2], writes=[bTA[a]])
                            P.op("act", ACTF(TMPA[:, a, 0:T], TMPA[:, a, 0:T], AF.Exp, scale=-0.5), reads=[bTA[a]], writes=[bTA[a]])
                            P.op("dve", STT(QK[:, qi, tok0:tok0 + T], ps[:, 0:T], QKG[:, j, qi:qi + 1], TMPA[:, a, 0:T], ALU.mult, ALU.mult),
                                 reads=[bps, bTA[a], bsmall], writes=[bQK])
                        pipe.tick()
                        pipe.defer(1, p2)
                pipe.flush()
                for g0 in range(0, 34, 4):
                    ng = min(4, 34 - g0)
                    ps, bps = psbank(0, 4)
                    fl = []
                    for gi in range(ng):
                        g = g0 + gi
                        fl += [MM(ps[:, gi * 128:(gi + 1) * 128], HT[:, k, g * 128:(g + 1) * 128], wv[:, k, :], k == 0, k == 7) for k in range(8)]
                    P.op("pe", fl, reads=[bWB[par]] + HTbufs(g0 * 128, ng * 128), writes=[bps])
                    P.op("act", ACTF(VP[:, g0:g0 + ng, :, 0:64], ps[:, 0:ng * 128].rearrange("p (g h e) -> p g h e", g=ng, h=2), AF.Copy),
                         reads=[bps], writes=[bV])
                pipe = Pipe()
                for rp in [0, 1, 2, 3, 32, 33] + list(range(4, 32)):
                    if rp == 4:
                        P.op("pool", MSET(WT[:, :, 11, :], -30000.0), writes=[bV])
                        P.op("pool", MSET(WT[0:64, :, 2, :], -30000.0), writes=[bV])
                        P.op("pool", MSET(WT[64:128, :, 10, :], -30000.0), writes=[bV])
                    po, bpo = psbank(4, 6)
                    qtok = rp * 128
                    for hh in range(2):
                        hs = slice(hh * 64, hh * 64 + 64)
                        si = nxt("ss", 2)
                        psS = PSALL[:, si * 1024:(si + 1) * 1024]
                        bS = [bPS[2 * si], bPS[2 * si + 1]]
                        fl = []
                        rsl = []
                        if rp >= 2:
                            m_ = rp - 2
                            rsl = [min(max(2 * m_ + a - 4, 0), 56) for a in range(2)]
                            j0 = rsl[0] // 2
                            assert rsl[1] // 2 == j0
                            nblk = 5 if (rsl[0] % 2 or rsl[1] % 2) else 4
                            i0 = 2 * j0 - 2 * m_ + 7
                            for b in range(nblk):
                                ktok = NCTX + (2 * (j0 + b)) * 64
                                fl.append(MM(psS[:, b * 128:(b + 1) * 128], QK[hs, 1, ktok:ktok + 128], QK[hs, 0, qtok:qtok + 128]))
                        else:
                            nblk = 0
                        nl = nblk * 128
                        for cb in range(2):
                            fl.append(MM(psS[:, nl + cb * 128:nl + (cb + 1) * 128], QK[hs, 1, cb * 128:(cb + 1) * 128], QK[hs, 0, qtok:qtok + 128]))
                        P.op("pe", fl, reads=[bQK], writes=bS)
                        ncol = nl + 256
                        if nblk:
                            sv = psS[:, 0:nl].rearrange("p (b a q) -> p b a q", b=nblk, a=2)
                            w0 = WT[:, hh, i0, :]
                            wap = bass.AP(tensor=w0.tensor, offset=w0.offset + 63, ap=[list(w0.ap[0]), [128, nblk], [-64, 2], [-1, 64]])
                            P.op("dve", TT(sv, sv, wap, ALU.add), reads=bS + [bV], writes=bS)
                        pi = nxt("pt", 2)
                        P.op("act", ACTF(PT[:, pi, 0:ncol], psS[:, 0:ncol], AF.Exp), reads=bS, writes=[bPT[pi]])
                        if nblk == 5:
                            assert 4 <= rp < 32 and i0 == 3
                        def stB(po=po, bpo=bpo, hh=hh, nblk=nblk, nl=nl, pi=pi, j0=(j0 if nblk else 0)):
                            fl = []
                            oreg = po[:, hh * 128:hh * 128 + 65]
                            for b in range(nblk):
                                fl.append(MM(oreg, PT[:, pi, b * 128:(b + 1) * 128], VP[:, 2 + j0 + b, hh, :], b == 0, False))
                            for cb in range(2):
                                fl.append(MM(oreg, PT[:, pi, nl + cb * 128:nl + (cb + 1) * 128], VP[:, cb, hh, :], (nblk + cb) == 0, cb == 1))
                            P.op("pe", fl, reads=[bPT[pi], bV], writes=[bpo])
                        pipe.tick()
                        pipe.defer(1, stB)
                    def stC(po=po, bpo=bpo, rp=rp):
                        ai = rp % 2
                        pov = po[:, 0:256].rearrange("p (h e) -> p h e", h=2)
                        P.op("dve", RCP(RC[:, ai, :].unsqueeze(2), pov[:, :, 64:65]), reads=[bpo], writes=[bRC[ai]])
                        P.op("dve", TT(AT[:, ai, :].rearrange("p (h e) -> p h e", h=2), pov[:, :, 0:64],
                                       RC[:, ai, :].unsqueeze(2).to_broadcast([128, 2, 64]), ALU.mult),
                             reads=[bpo, bRC[ai]], writes=[bAT[ai]])

                    def stD(rp=rp):
                        ai = rp % 2
                        tr = nxt("tr", 8)
                        P.op("pe", lambda e, tr=tr, ai=ai: e.transpose(PTR[:, tr * 128:(tr + 1) * 128], AT[:, ai, :], ident[:]),
                             reads=[bAT[ai], bconst], writes=[bPTR[tr]])
                        P.op("act", ACTF(YC[:, rp * 128:(rp + 1) * 128], PTR[:, tr * 128:(tr + 1) * 128], AF.Copy), reads=[bPTR[tr]], writes=[bYC])
                    pipe.defer(1, stC)
                    pipe.defer(2, stD)
                pipe.flush()
                P.dma("sp", DMA(YT[c], YC), reads=[bYC], writes=[bYT], waw=False, semgrp="stYC%d" % par)

        def emit_lru(i, j):
            R = LN[:, 0, :]
            XR = LN[:, 1, :]
            OM = LN[:, 2, :]
            IX = LN[:, 3, :]
            HS = LN[:, 4, :]
            XSb = XS[:].rearrange("p a b -> p (a b)").bitcast(BF16)
            GG = XSb[:, 0:NTOK]
            XRb = XSb[:, NTOK:2 * NTOK]
            bGG, bXRb = bXS[0], bXS[1]
            for li in (0, 1):
                for (c0, c1) in ((0, 2), (258, 260), (4356, 4358)):
                    P.op("pool", MSET(LN[:, li, c0:c1], 0.0), writes=[bL[li]])
            def lviews(c):
                par = c % 2
                wg = WB[:, par, 0:1024].rearrange("p (k n) -> p k n", k=8)
                wr = WB[:, par, 1024:2048].rearrange("p (k n) -> p k n", k=8)
                bdw = WB[:, par, 2048:2560].rearrange("p (d g n) -> p d g n", d=2, g=2)
                return par, wg, wr, bdw

            def lload(c):
                par, wg, wr, bdw = lviews(c)
                load_w(wview(w_lin[j], c * 128, 128), wg, [bWB[par]], 1024)
                load_w(wview(w_lin[j], D + c * 128, 128), wr, [bWB[par]], 1024)
                load_w(lbd[:, j, :, :, c, :], bdw, [bWB[par]], 512)

            lload(0)
            for c in range(8):
                if c + 1 < 8:
                    lload(c + 1)
                par, wg, wr, bdw = lviews(c)
                for (tok0, T, kind) in TILES512:
                    ps, bps = psbank(0, 6)
                    P.op("pe", [MM(ps[:, 0:T], wr[:, k, :], HT[:, k, tok0:tok0 + T], k == 0, k == 7) for k in range(8)],
                         reads=[bWB[par]] + HTbufs(tok0, T), writes=[bps])
                    P.op("act", ACTF(R[:, colp(tok0):colp(tok0) + T], ps[:, 0:T], AF.Copy), reads=[bps], writes=[bL[0]])
                    ps, bps = psbank(0, 6)
                    P.op("pe", [MM(ps[:, 0:T], wg[:, k, :], HT[:, k, tok0:tok0 + T], k == 0, k == 7) for k in range(8)],
                         reads=[bWB[par]] + HTbufs(tok0, T), writes=[bps])
                    P.op("act", ACTF(GG[:, tok0:tok0 + T], ps[:, 0:T], AF.Gelu_apprx_tanh), reads=[bps], writes=[bGG])
                n = LW - 4
                P.op("dve", TS(XR[:, 2:2 + n], R[:, 0:n], LCW[:, j, 0, c:c + 1], LCB[:, j, c:c + 1], ALU.mult, ALU.add),
                     reads=[bL[0], bsmall], writes=[bL[1]])
                for tap in (1, 2, 3):
                    P.op("dve", STT(XR[:, 2:2 + n], R[:, tap:tap + n], LCW[:, j, tap, c:c + 1], XR[:, 2:2 + n], ALU.mult, ALU.add),
                         reads=[bL[0], bL[1], bsmall], writes=[bL[1]])
                P.op("pool", CP(XRb[:, 0:NCTX], XR[:, 2:2 + NCTX]), reads=[bL[1]], writes=[bXRb])
                P.op("pool", CP(XRb[:, NCTX:NTOK], XR[:, 260:260 + 4096]), reads=[bL[1]], writes=[bXRb])
                A = R
                for d in range(2):
                    for (tok0, T, kind) in TILES512:
                        cp0 = colp(tok0)
                        psr, bpr = psbank(0, 6)
                        P.op("pe", MM(psr[:, 0:T], bdw[:, d, 0, :], XRb[:, tok0:tok0 + T]), reads=[bWB[par], bXRb], writes=[bpr])
                        psi, bpi = psbank(0, 6)
                        P.op("pe", MM(psi[:, 0:T], bdw[:, d, 1, :], XRb[:, tok0:tok0 + T]), reads=[bWB[par], bXRb], writes=[bpi])
                        a = nxt("ta", 2)
                        a2 = nxt("ta", 2)
                        P.op("act", ACTF(TMPA[:, a, 0:T], psr[:, 0:T], AF.Tanh, bias=LGBH[:, j, d, 0, c:c + 1], scale=0.5),
                             reads=[bpr, bsmall], writes=[bTA[a]])
                        P.op("act", ACTF(TMPA[:, a2, 0:T], psi[:, 0:T], AF.Tanh, bias=LGBH[:, j, d, 1, c:c + 1], scale=0.5),
                             reads=[bpi, bsmall], writes=[bTA[a2]])
                        P.op("act", ACTF(A[:, cp0:cp0 + T], TMPA[:, a, 0:T], AF.Exp, bias=LC8[:, j, d, c, 0:1], scale=LC8[:, j, d, c, 0:1]),
                             reads=[bTA[a], bsmall], writes=[bL[0]])
                        P.op("act", ACTF(OM[:, cp0:cp0 + T], TMPA[:, a, 0:T], AF.Exp, bias=LC8[:, j, d, c, 1:2], scale=LC8[:, j, d, c, 1:2]),
                             reads=[bTA[a], bsmall], writes=[bL[2]])
                        P.op("dve", STT(IX[:, cp0:cp0 + T], TMPA[:, a2, 0:T], 1.0, XR[:, cp0:cp0 + T], ALU.add, ALU.mult),
                             reads=[bTA[a2], bL[1]], writes=[bL[3]])
                    P.op("act", ACTF(OM[:, 2:LW - 2], OM[:, 2:LW - 2], AF.Sqrt, bias=1.0, scale=-1.0), reads=[bL[2]], writes=[bL[2]])
                    P.op("dve", STT(IX[:, 2:LW - 2], IX[:, 2:LW - 2], 0.5, OM[:, 2:LW - 2], ALU.mult, ALU.mult),
                         reads=[bL[3], bL[2]], writes=[bL[3]])
                    dst = HS if d == 0 else OM
                    bdst = bL[4] if d == 0 else bL[2]
                    cs, ls = slice(2, 258), slice(260, 4356)
                    if d == 0:
                        P.op("dve", SCAN(dst[:, cs], A[:, cs], IX[:, cs], 0.0), reads=[bL[0], bL[3]], writes=[bdst])
                        P.op("dve", SCAN(dst[:, ls], A[:, ls], IX[:, ls], dst[:, 257:258]), reads=[bL[0], bL[3], bdst], writes=[bdst])
                    else:
                        rcs, rls = slice(257, 1, -1), slice(4355, 259, -1)
                        P.op("dve", SCAN(dst[:, rcs], A[:, rcs], IX[:, rcs], 0.0), reads=[bL[0], bL[3]], writes=[bdst])
                        P.op("dve", SCAN(dst[:, rls], A[:, rls], IX[:, rls], dst[:, 2:3]), reads=[bL[0], bL[3], bdst], writes=[bdst])
                for (s0, s1, t0, t1) in ((2, 258, 0, 256), (260, 4356, 256, NTOK)):
                    P.op("dve", TT(HS[:, s0:s1], HS[:, s0:s1], OM[:, s0:s1], ALU.add), reads=[bL[4], bL[2]], writes=[bL[4]])
                    P.op("dve", TT(GG[:, t0:t1], HS[:, s0:s1], GG[:, t0:t1], ALU.mult), reads=[bL[4], bGG], writes=[bGG])
                P.dma("sp", DMA(YT[c], GG), reads=[bGG], writes=[bYT], waw=False, semgrp="stXS0")

        def emit_m3(i, j, is_na):
            Xsrc, bXsrc = (xin, None) if i == 0 else (XA, bXA)
            WO = LN[:, 0, :].bitcast(BF16)[:, 0:8192].rearrange("p (k n) -> p k n", k=8)
            wsrc = (w_nao if is_na else w_lout)[j]
            for k in range(8):
                load_w(wsrc[k * 128:(k + 1) * 128, :], WO[:, k, :], [bL[0]], 1024)
            YL = LN[:, 2, :].bitcast(BF16)
            for ti, (tok0, T, kind) in enumerate(TILES256):
                xt = xslot(ti)
                yt = YL[:, (ti % 2) * 2048:(ti % 2) * 2048 + 2048].rearrange("p (k t) -> p k t", k=8)
                P.dma("sp", DMA(xt, xsrc(Xsrc, tok0, T)), reads=([bXsrc] if bXsrc else []), writes=[bXS[ti % 2]])
                P.dma("sp", DMA(yt, YT[:, :, tok0:tok0 + T].rearrange("k p t -> p k t")), reads=[bYT], writes=[bL[2]])
                for m in range(8):
                    ps, bps = psbank(0, 6)
                    P.op("pe", [MM(ps[:, 0:T], WO[:, k, m * 128:(m + 1) * 128], yt[:, k, :], k == 0, k == 7) for k in range(8)],
                         reads=[bL[0], bL[2]], writes=[bps])
                    P.op("dve", STT(xt[:, m, :], ps[:, 0:T], der(i, 2, kind, m), xt[:, m, :], ALU.mult, ALU.add),
                         reads=[bps, bDER[i % 2], bXS[ti % 2]], writes=[bXS[ti % 2]])
                P.dma("sp", DMA(xsrc(XB, tok0, T), xt), reads=[bXS[ti % 2]], writes=[bXB], waw=False, semgrp="stXS%d" % (ti % 2))
                emit_norm(xt, ti, i, 3, 4)

        def emit_f1(i, extra=None):
            xd, xm, xfin = extra if extra else ([], [], None)
            pend = []
            P.dma("sp", [DMA(FCW[:], fcw[:, i]), DMA(FCB[:], fcb[:, i])], writes=[bFC])

            def fload(c):
                wu_ = WB[:, c % 2, 0:2048].rearrange("p (w k n) -> p w k n", w=2, k=8)
                load_w(wview(w_up[i], c * 128, 128), wu_[:, 0], [bWB[c % 2]], 1024)
                load_w(wview(w_up[i], 3 * D + c * 128, 128), wu_[:, 1], [bWB[c % 2]], 1024)

            fload(0)
            for c in range(24):
                for f in pend:
                    f()
                pend = []
                if c + 1 < 24:
                    fload(c + 1)
                for _ in range(2):
                    if xd:
                        xd.pop(0)()
                        pend.append(xm.pop(0))
                par = c % 2
                wu = WB[:, par, 0:2048].rearrange("p (w k n) -> p w k n", w=2, k=8)
                accs = ((LN[:, 2 * par, :], bL[2 * par]), (LN[:, 2 * par + 1, :], bL[2 * par + 1]))
                AV, AG = accs[0][0], accs[1][0]
                Gst = LN[:, 4, :].bitcast(BF16)[:, par * NTOK:(par + 1) * NTOK]
                for w, (ACC, bACC) in enumerate(accs):
                    ch = w * 24 + c
                    for c0_ in (2, 260):
                        P.op("act", ACTF(ACC[:, c0_:c0_ + 1], FCB[:, ch:ch + 1], AF.Identity), reads=[bFC], writes=[bACC])
                for (tok0, T, kind) in TILES512:
                    cp0 = colp(tok0)
                    for w, (ACC, bACC) in enumerate(accs):
                        ch = w * 24 + c
                        ps, bps = psbank(0, 6)
                        P.op("pe", [MM(ps[:, 0:T], wu[:, w, k, :], HT[:, k, tok0:tok0 + T], k == 0, k == 7) for k in range(8)],
                             reads=[bWB[par]] + HTbufs(tok0, T), writes=[bps])
                        P.op("act", ACTF(ACC[:, cp0 + 1:cp0 + 1 + T], ps[:, 0:T], AF.Identity, bias=FCB[:, ch:ch + 1], scale=FCW[:, 0, ch:ch + 1]),
                             reads=[bps, bFC], writes=[bACC])
                        P.op("dve", STT(ACC[:, cp0:cp0 + T], ps[:, 0:T], FCW[:, 1, ch:ch + 1], ACC[:, cp0:cp0 + T], ALU.mult, ALU.add),
                             reads=[bps, bFC, bACC], writes=[bACC])
                        P.op("dve", STT(ACC[:, cp0 - 1:cp0 - 1 + T], ps[:, 0:T], FCW[:, 2, ch:ch + 1], ACC[:, cp0 - 1:cp0 - 1 + T], ALU.mult, ALU.add),
                             reads=[bps, bFC, bACC], writes=[bACC])
                P.op("act", ACTF(AG[:, 2:LW - 2], AG[:, 2:LW - 2], AF.Silu), reads=[accs[1][1]], writes=[accs[1][1]])
                for (s0, s1, t0, t1) in ((2, 258, 0, 256), (260, 4356, 256, NTOK)):
                    P.op("pool", TT(Gst[:, t0:t1], AV[:, s0:s1], AG[:, s0:s1], ALU.mult), reads=[accs[0][1], accs[1][1]], writes=[bL[4]])
                P.dma("sp", DMA(GS[c], Gst), reads=[bL[4]], writes=[bGS], waw=False, semgrp="stL4_%d" % par)
            for f in pend:
                f()
            assert not xd
            if xfin:
                xfin()

        def emit_f2(i, last):
            WDn = LN[:, 0:3, :].rearrange("p a b -> p (a b)").bitcast(BF16)[:, 0:24 * 1024].rearrange("p (k n) -> p k n", k=24)
            for k in range(24):
                load_w(w_down[i][k * 128:(k + 1) * 128, :], WDn[:, k, :], [bL[0], bL[1], bL[2]], 1024)
            for ti, (tok0, T, kind) in enumerate(TILES256):
                if last and kind == 1:
                    continue
                xt = xslot(ti)
                gt = LN[:, 3 + ti % 2, :].bitcast(BF16)[:, 0:24 * 256].rearrange("p (k t) -> p k t", k=24)
                P.dma("sp", DMA(xt, xsrc(XB, tok0, T)), reads=[bXB], writes=[bXS[ti % 2]])
                P.dma("sp", DMA(gt, GS[:, :, tok0:tok0 + T].rearrange("k p t -> p k t")), reads=[bGS], writes=[bL[3 + ti % 2]])
                for m in range(8):
                    ps, bps = psbank(0, 6)
                    P.op("pe", [MM(ps[:, 0:T], WDn[:, k, m * 128:(m + 1) * 128], gt[:, k, :], k == 0, k == 23) for k in range(24)],
                         reads=[bL[0], bL[1], bL[2], bL[3 + ti % 2]], writes=[bps])
                    P.op("dve", STT(xt[:, m, :], ps[:, 0:T], der(i, 5, kind, m), xt[:, m, :], ALU.mult, ALU.add),
                         reads=[bps, bDER[i % 2], bXS[ti % 2]], writes=[bXS[ti % 2]])
                if last:
                    P.dma("sp", DMA(xsrc(out, tok0 - NCTX, T), xt), reads=[bXS[ti % 2]], writes=[bOUT], waw=False, semgrp="stXS%d" % (ti % 2))
                else:
                    P.dma("sp", DMA(xsrc(XA, tok0, T), xt), reads=[bXS[ti % 2]], writes=[bXA], waw=False, semgrp="stXS%d" % (ti % 2))
                    emit_norm(xt, ti, i + 1, 0, 1)

        emit_mods(0)
        for ti, (tok0, T, kind) in enumerate(TILES256):
            xt = xslot(ti)
            P.dma("sp", DMA(xt, xsrc(xin, tok0, T)), writes=[bXS[ti % 2]])
            emit_norm(xt, ti, 0, 0, 1)
        for i in range(nlayers):
            j = i // 2
            is_na = (i % 2 == 0)
            if is_na:
                emit_na(i, j)
            else:
                emit_lru(i, j)
            emit_m3(i, j, is_na)
            emit_f1(i, mods_items(i + 1) if i + 1 < nlayers else None)
            emit_f2(i, i == nlayers - 1)
        P.final_wait("sp", [bOUT])
        P.emit()
    return nc


_CACHE = {}


def _prep_shared(inp):
    f = np.float32

    def pcol(v, lead):
        v = np.asarray(v, f)
        v = v.reshape(lead + (8, 128))
        return np.ascontiguousarray(np.moveaxis(v, -1, 0))
    sh = {}
    sh["ada_w"] = np.ascontiguousarray(inp["ada_w"], f)
    sh["ada_b"] = np.ascontiguousarray(np.asarray(inp["ada_b"], f).reshape(DEPTH, 48, 128).transpose(2, 0, 1))
    sh["nmix"] = pcol(inp["norm_mix"], (DEPTH,))
    sh["nffn"] = pcol(inp["norm_ffn"], (DEPTH,))
    sh["w_qkv"] = np.ascontiguousarray(inp["na_w_qkv"], f)
    qg = np.asarray(inp["na_q_gain"], f)
    kg = np.asarray(inp["na_k_gain"], f)
    qkg = np.zeros((128, 2, 2), f)
    for jj in range(2):
        qkg[:, jj, 0] = np.tile(qg[jj], 2)
        qkg[:, jj, 1] = np.tile(kg[jj], 2)
    sh["qkg"] = qkg
    rp = np.zeros((2, 16, 15, 127), f)
    rp[:, :, :, 48:79] = np.asarray(inp["na_rpb"], f)
    sh["rpbp"] = rp
    sh["w_nao"] = np.ascontiguousarray(inp["na_w_out"], f)
    sh["w_lin"] = np.ascontiguousarray(inp["lru_w_in"], f)
    sh["lcw"] = pcol(inp["lru_conv_w"], (2, 4))
    sh["lcb"] = pcol(inp["lru_conv_b"], (2,))
    ga = np.asarray(inp["lru_ga_w"], f)
    gx = np.asarray(inp["lru_gx_w"], f)
    lbd = np.zeros((128, 2, 2, 2, 8, 128), f)
    for g, w in enumerate((ga, gx)):
        for jj in range(2):
            for d in range(2):
                for c in range(8):
                    for hb in range(2):
                        lbd[hb * 64:(hb + 1) * 64, jj, d, g, c, hb * 64:(hb + 1) * 64] = w[jj, d, 2 * c + hb]
    sh["lbd"] = lbd
    gab = pcol(inp["lru_ga_b"], (2, 2))
    gxb = pcol(inp["lru_gx_b"], (2, 2))
    sh["lgb"] = np.ascontiguousarray(np.stack([gab, gxb], axis=3))
    sh["llam"] = pcol(inp["lru_lambda"], (2, 2))
    sh["w_lout"] = np.ascontiguousarray(inp["lru_w_out"], f)
    sh["w_up"] = np.ascontiguousarray(inp["ffn_w_up"], f)
    sh["fcw"] = np.ascontiguousarray(np.asarray(inp["ffn_conv_w"], f).reshape(DEPTH, 3, 48, 128).transpose(3, 0, 1, 2))
    sh["fcb"] = np.ascontiguousarray(np.asarray(inp["ffn_conv_b"], f).reshape(DEPTH, 48, 128).transpose(2, 0, 1))
    sh["w_down"] = np.ascontiguousarray(inp["ffn_w_down"], f)
    bf = ml_dtypes.bfloat16
    sh["c_ident"] = np.eye(128, dtype=f).astype(bf)
    bo = np.zeros((128, 128), f)
    bo[:64, :64] = 1
    bo[64:, 64:] = 1
    sh["c_bones"] = bo.astype(bf)
    sh["c_ones"] = np.ones((128, 128), f).astype(bf)
    kc = np.arange(64)[:, None]
    qc = 63 - np.arange(64)[None, :]
    ws = np.clip(qc - 8, 0, 48)
    ok = (kc >= ws) & (kc < ws + 16)
    cmv = np.where(ok, 0.0, -30000.0).astype(f)
    sh["c_cm"] = np.ascontiguousarray(np.concatenate([cmv, cmv], 0))
    return sh


def kernel(nlayers=DEPTH, **inp):
    if nlayers not in _CACHE:
        _CACHE[nlayers] = build_program(nlayers)
    nc = _CACHE[nlayers]
    sh = _prep_shared(inp)
    x = np.asarray(inp["x"], np.float32)
    ctx = np.asarray(inp["ctx"], np.float32)
    c = np.asarray(inp["c"], np.float32)
    cc = np.asarray(inp["c_ctx"], np.float32)
    in_maps = []
    for b in range(8):
        m = dict(sh)
        m["xin"] = np.ascontiguousarray(np.concatenate([ctx[b], x[b]], 0).T)
        col = np.stack([c[b].reshape(8, 128).T, cc.reshape(8, 128).T], axis=2)
        m["ccol"] = np.ascontiguousarray(col, np.float32)
        in_maps.append(m)
    res = run_bass_kernel_spmd(nc, in_maps, core_ids=list(range(8)))
    outs = [np.ascontiguousarray(res.results[b]["out"].T) for b in range(8)]
    return np.stack(outs, 0).astype(np.float32)
```
